# Optimizing a Trainium2 kernel written in Bass

```python
import math
import jax, jax.numpy as jnp
from jax import lax
import numpy as np

D_MODEL = 1024
BATCH = 8
SEQ = 2048
DEPTH = 1
DEC_BATCH = 128
DEC_SEQ = 4
PAST_LEN = 8192
PAGE_SIZE = 128

N_META = 16
D_POOL = D_MODEL // 2
N_POOL_GROUPS = 4
POOL_GROUP_DIM = D_POOL // N_POOL_GROUPS
POOL_WINDOWS = (2, 4, 8, 16)
POOL_BUF = min(max(POOL_WINDOWS) - 1, PAST_LEN)
N_HEADS = 8
HEAD_DIM = 64
D_ATTN = N_HEADS * HEAD_DIM
N_KV_HEADS = 2
GROUP = N_HEADS // N_KV_HEADS
WINDOW = 128
BLOCK = 128
WIN_BUF = min(WINDOW, PAST_LEN)
REL_BUCKETS = 32
REL_MAX_DIST = 128
D_MIX = D_POOL + D_ATTN
D_KV = N_KV_HEADS * HEAD_DIM
D_IN_PROJ = D_POOL + D_ATTN + 2 * D_KV
D_FF = 4 * D_MODEL
ALPHA = (2.0 * DEPTH) ** 0.25
BETA = (8.0 * DEPTH) ** -0.25
LN_EPS = 1e-5

kernel_name = "hymba_pool_swa_sink_decoder_step"


def layer_norm(x, g, b):
    xf = x.astype(jnp.float32)
    mu = jnp.mean(xf, axis=-1, keepdims=True)
    var = jnp.mean(jnp.square(xf - mu), axis=-1, keepdims=True)
    return ((xf - mu) * lax.rsqrt(var + LN_EPS) * g.astype(jnp.float32) + b.astype(jnp.float32)).astype(x.dtype)


def split_in_proj(h, w_in):
    B, T = h.shape[:2]
    proj = jnp.einsum('btd,de->bte', h, w_in)
    u, q, k, v = jnp.split(proj, [D_POOL, D_POOL + D_ATTN, D_POOL + D_ATTN + D_KV], axis=-1)
    return (u, q.reshape(B, T, N_HEADS, HEAD_DIM),
            k.reshape(B, T, N_KV_HEADS, HEAD_DIM), v.reshape(B, T, N_KV_HEADS, HEAD_DIM))


def pool_mix(u, pos, w_pool, scale):
    T = u.shape[1]
    c = jnp.cumsum(u.astype(jnp.float32), axis=1)
    c = jnp.concatenate([jnp.zeros_like(c[:, :1]), c], axis=1)
    idx = jnp.arange(T)
    outs = []
    for g, w in enumerate(POOL_WINDOWS):
        cg = c[:, :, g * POOL_GROUP_DIM:(g + 1) * POOL_GROUP_DIM]
        lo = jnp.maximum(idx + 1 - w, 0)
        s = cg[:, 1:] - jnp.take(cg, lo, axis=1)
        cnt = jnp.minimum(w, pos + 1).astype(jnp.float32)[None, :, None]
        ug = u[:, :, g * POOL_GROUP_DIM:(g + 1) * POOL_GROUP_DIM].astype(jnp.float32)
        outs.append(s / cnt - ug)
    p = jnp.stack(outs, axis=2).astype(u.dtype)
    z = jnp.einsum('btgc,gcd->btgd', p, w_pool)
    return z.reshape(u.shape) * scale


def rel_bias(dist, table):
    n = jnp.maximum(dist, 0)
    max_exact = REL_BUCKETS // 2
    nf = jnp.maximum(n, 1).astype(jnp.float32)
    large = max_exact + (jnp.log(nf / max_exact) / math.log(REL_MAX_DIST / max_exact)
                         * (REL_BUCKETS - max_exact)).astype(jnp.int32)
    large = jnp.minimum(large, REL_BUCKETS - 1)
    bucket = jnp.where(n < max_exact, n, large)
    return jnp.moveaxis(table[bucket].astype(jnp.float32), -1, 0)


def sink_attend(q, k, v, bias, mask, sinks):
    s = jnp.einsum('bnqhgd,bnkhd->bnhgqk', q, k, preferred_element_type=jnp.float32) * (HEAD_DIM ** -0.5)
    s = s + bias.reshape(N_KV_HEADS, GROUP, bias.shape[1], bias.shape[2])
    s = jnp.where(mask[None, :, None, None], s, -jnp.inf)
    sink = sinks.astype(jnp.float32).reshape(1, 1, N_KV_HEADS, GROUP, 1, 1)
    m = jnp.maximum(jnp.max(s, axis=-1, keepdims=True), sink)
    p = jnp.exp(s - m)
    p = p / (jnp.sum(p, axis=-1, keepdims=True) + jnp.exp(sink - m))
    return jnp.einsum('bnhgqk,bnkhd->bnqhgd', p.astype(v.dtype), v)


def prompt_attention(q, k, v, table, sinks):
    B, L = q.shape[:2]
    pad = BLOCK - N_META
    Lp = L + pad
    nb = Lp // BLOCK

    def padf(t):
        return jnp.pad(t, ((0, 0), (pad, 0)) + ((0, 0),) * (t.ndim - 2))

    def prev(t):
        return jnp.concatenate([jnp.zeros_like(t[:, :1]), t[:, :-1]], axis=1)

    qb = padf(q).reshape(B, nb, BLOCK, N_KV_HEADS, GROUP, HEAD_DIM)
    kb = padf(k).reshape(B, nb, BLOCK, N_KV_HEADS, HEAD_DIM)
    vb = padf(v).reshape(B, nb, BLOCK, N_KV_HEADS, HEAD_DIM)
    kk = jnp.concatenate([prev(kb), kb], axis=2)
    vv = jnp.concatenate([prev(vb), vb], axis=2)
    dist = (jnp.arange(BLOCK)[:, None] + BLOCK) - jnp.arange(2 * BLOCK)[None, :]
    key_pos = (jnp.arange(nb)[:, None] * BLOCK - BLOCK + jnp.arange(2 * BLOCK)[None, :]) - pad
    mask = ((dist >= 0) & (dist < WINDOW))[None] & (key_pos >= 0)[:, None, :]
    o = sink_attend(qb, kk, vv, rel_bias(dist, table), mask, sinks)
    return o.reshape(B, Lp, D_ATTN)[:, pad:]


def sample_attention(q, k, v, k_cache, v_cache, table, sinks):
    DB, T = q.shape[:2]
    kk = jnp.concatenate([k_cache, k], axis=1)
    vv = jnp.concatenate([v_cache, v], axis=1)
    dist = (jnp.arange(T)[:, None] + WIN_BUF) - jnp.arange(WIN_BUF + T)[None, :]
    mask = ((dist >= 0) & (dist < WINDOW))[None]
    o = sink_attend(q.reshape(DB, 1, T, N_KV_HEADS, GROUP, HEAD_DIM), kk[:, None], vv[:, None],
                    rel_bias(dist, table), mask, sinks)
    return o.reshape(DB, T, D_ATTN), kk[:, -WIN_BUF:], vv[:, -WIN_BUF:]


def finish_layer(h, z_pool, o_attn, w_out, ln1_g, ln1_b, w_mlp_in, w_mlp_out, ln2_g, ln2_b):
    mix = jnp.einsum('bte,ed->btd', jnp.concatenate([z_pool, o_attn], axis=-1), w_out)
    h = layer_norm(ALPHA * h + mix, ln1_g, ln1_b)
    f = jnp.einsum('btf,fd->btd', jnp.square(jax.nn.relu(jnp.einsum('btd,df->btf', h, w_mlp_in))), w_mlp_out)
    return layer_norm(ALPHA * h + f, ln2_g, ln2_b)


def setup_inputs(seed: int = 0) -> dict:
    key = jax.random.key(seed)
    ks = jax.random.split(key, 24)

    def nrm(k, shape, s=1.0):
        return jax.random.normal(k, shape, jnp.float32) * s

    return {
        "x_prompt": nrm(ks[0], (BATCH, SEQ, D_MODEL)),
        "x_sample": nrm(ks[1], (DEC_BATCH, DEC_SEQ, D_MODEL)),
        "cache_win_k": nrm(ks[2], (DEPTH, DEC_BATCH, WIN_BUF, N_KV_HEADS, HEAD_DIM)),
        "cache_win_v": nrm(ks[3], (DEPTH, DEC_BATCH, WIN_BUF, N_KV_HEADS, HEAD_DIM)),
        "state_pool": nrm(ks[4], (DEPTH, DEC_BATCH, POOL_BUF, D_POOL)),
        "meta_tokens": nrm(ks[5], (N_META, D_MODEL)),
        "ln_emb_g": 1.0 + nrm(ks[6], (D_MODEL,), 0.1),
        "ln_emb_b": nrm(ks[7], (D_MODEL,), 0.02),
        "rel_table": nrm(ks[8], (REL_BUCKETS, N_HEADS), 0.5),
        "w_in": nrm(ks[9], (DEPTH, D_MODEL, D_IN_PROJ), D_MODEL ** -0.5),
        "w_pool": nrm(ks[10], (DEPTH, N_POOL_GROUPS, POOL_GROUP_DIM, POOL_GROUP_DIM), POOL_GROUP_DIM ** -0.5),
        "pool_scale": 1.0 + nrm(ks[11], (DEPTH, D_POOL), 0.1),
        "sinks": nrm(ks[12], (DEPTH, N_HEADS), 0.5),
        "w_out": nrm(ks[13], (DEPTH, D_MIX, D_MODEL), BETA * D_MIX ** -0.5),
        "ln1_g": 1.0 + nrm(ks[14], (DEPTH, D_MODEL), 0.1),
        "ln1_b": nrm(ks[15], (DEPTH, D_MODEL), 0.02),
        "w_mlp_in": nrm(ks[16], (DEPTH, D_MODEL, D_FF), D_MODEL ** -0.5),
        "w_mlp_out": nrm(ks[17], (DEPTH, D_FF, D_MODEL), BETA * D_FF ** -0.5),
        "ln2_g": 1.0 + nrm(ks[18], (DEPTH, D_MODEL), 0.1),
        "ln2_b": nrm(ks[19], (DEPTH, D_MODEL), 0.02),
    }


def reference(x_prompt, x_sample, cache_win_k, cache_win_v, state_pool, meta_tokens, ln_emb_g, ln_emb_b,
              rel_table, w_in, w_pool, pool_scale, sinks, w_out, ln1_g, ln1_b, w_mlp_in, w_mlp_out,
              ln2_g, ln2_b):
    B = x_prompt.shape[0]
    meta = jnp.broadcast_to(meta_tokens[None].astype(x_prompt.dtype), (B, N_META, D_MODEL))
    hp = layer_norm(jnp.concatenate([meta, x_prompt], axis=1), ln_emb_g, ln_emb_b)
    hs = layer_norm(x_sample, ln_emb_g, ln_emb_b)
    T = hs.shape[1]
    pos_p = jnp.arange(hp.shape[1])
    pos_s = PAST_LEN - POOL_BUF + jnp.arange(POOL_BUF + T)

    nk_p, nv_p, np_p, nk_s, nv_s, np_s = [], [], [], [], [], []
    for l in range(DEPTH):
        u, q, k, v = split_in_proj(hp, w_in[l])
        z_p = pool_mix(u, pos_p, w_pool[l], pool_scale[l])
        o_p = prompt_attention(q, k, v, rel_table, sinks[l])
        nk_p.append(k[:, -WIN_BUF:])
        nv_p.append(v[:, -WIN_BUF:])
        np_p.append(u[:, -POOL_BUF:])
        hp = finish_layer(hp, z_p, o_p, w_out[l], ln1_g[l], ln1_b[l], w_mlp_in[l], w_mlp_out[l], ln2_g[l], ln2_b[l])
        u, q, k, v = split_in_proj(hs, w_in[l])
        u_ext = jnp.concatenate([state_pool[l].astype(u.dtype), u], axis=1)
        z_s = pool_mix(u_ext, pos_s, w_pool[l], pool_scale[l])[:, POOL_BUF:]
        o_s, k_buf, v_buf = sample_attention(q, k, v, cache_win_k[l].astype(k.dtype), cache_win_v[l].astype(v.dtype),
                                             rel_table, sinks[l])
        nk_s.append(k_buf)
        nv_s.append(v_buf)
        np_s.append(u_ext[:, -POOL_BUF:])
        hs = finish_layer(hs, z_s, o_s, w_out[l], ln1_g[l], ln1_b[l], w_mlp_in[l], w_mlp_out[l], ln2_g[l], ln2_b[l])

    y_prompt = hp[:, N_META:]
    return (y_prompt, hs, jnp.stack(nk_p), jnp.stack(nv_p), jnp.stack(np_p),
            jnp.stack(nk_s), jnp.stack(nv_s), jnp.stack(np_s))
```

```python
import math
import numpy as np
import concourse.bass as bass
import concourse.mybir as mybir
from concourse.bass_utils import run_bass_kernel_spmd

F32 = mybir.dt.float32
BF16 = mybir.dt.bfloat16
AF = mybir.ActivationFunctionType
ALU = mybir.AluOpType

D = 1024
NTILE = 16
SEQ = 2048
NS = 64
NB = 16
TS = 4
ALPHA = float(2.0 ** 0.25)
EPS = 1e-5
MASK = -30000.0
import os as _osx
LIST_SCHED = _osx.environ.get("KLIST", "1") == "1"
LS_WINDOW = int(_osx.environ.get("KLWIN", "100"))
LS_PRIO = _osx.environ.get("KLPRIO", "order")
LS_HOP = float(_osx.environ.get("KLHOP", "0.4"))
SELF_ORDERED = ("pe",)


class Res:
    __slots__ = ("name", "w", "readers")

    def __init__(self, name):
        self.name = name
        self.w = None
        self.readers = []


class Op:
    __slots__ = ("eng", "fn", "deps", "stream", "signal", "sigval", "gidx", "cost", "gstart", "gstop")

    def __init__(self, eng, fn, deps, stream):
        self.eng = eng
        self.fn = fn
        self.deps = deps
        self.stream = stream
        self.signal = stream is not None
        self.sigval = None
        self.cost = 0.3
        self.gstart = True
        self.gstop = True


class Sched:
    ENGS = ("pe", "act", "dve", "pool", "sp")

    def __init__(self):
        self.q = {e: [] for e in self.ENGS}
        self.stream_ops = {}
        self.fence_deps = {e: set() for e in self.ENGS}
        self.n = 0

    def op(self, eng, fn, reads=(), writes=(), stream=None, free_writes=(), cost=None, gstart=True, gstop=True):
        deps = set()
        for r in reads:
            if r.w is not None:
                deps.add(r.w)
        for w in writes:
            if w.w is not None:
                deps.add(w.w)
            deps.update(w.readers)
        if self.fence_deps[eng]:
            deps.update(self.fence_deps[eng])
            self.fence_deps[eng] = set()
        o = Op(eng, fn, deps, stream)
        if cost is not None:
            o.cost = cost
        o.gstart = gstart
        o.gstop = gstop
        o.gidx = self.n
        self.n += 1
        self.q[eng].append(o)
        if stream is not None:
            self.stream_ops.setdefault(stream, []).append(o)
        for r in reads:
            r.readers.append(o)
        for w in writes:
            w.w = o
            w.readers = []
        for w in free_writes:
            w.w = o
            w.readers = []
        return o

    def fence(self, only=None):
        last = set()
        for e in self.ENGS:
            for o in reversed(self.q[e]):
                if o.stream is None:
                    last.add(o)
                    break
        for s, lst in self.stream_ops.items():
            last.add(lst[-1])
        for e in (only or self.ENGS):
            self.fence_deps[e] = set(last)

    def list_schedule(self, engines=("pe", "act", "dve"), window=48, hop=0.15):
        units = {}
        for e in self.ENGS:
            us = []
            cur = None
            for o in self.q[e]:
                if e == "pe":
                    if cur is None:
                        cur = [o]
                    else:
                        cur.append(o)
                    if o.gstop:
                        us.append(cur)
                        cur = None
                else:
                    us.append([o])
            if cur:
                us.append(cur)
            units[e] = us
        succ = {}
        allops = [o for e in self.ENGS for o in self.q[e]]
        for o in allops:
            for d in o.deps:
                succ.setdefault(d, []).append(o)
        cpl = {}
        for o in sorted(allops, key=lambda o: -o.gidx):
            m = 0.0
            for s_ in succ.get(o, ()):
                m = max(m, cpl[s_] + (hop if s_.eng != o.eng else 0.0))
            cpl[o] = o.cost + m
        fin = {}
        free = {e: 0.0 for e in self.ENGS}
        out = {e: [] for e in self.ENGS}
        remaining = sum(len(u) for u in units.values())

        def ready_time(unit, e):
            t = 0.0
            inside = set(unit)
            for o in unit:
                for d in o.deps:
                    if d in inside:
                        continue
                    if d not in fin:
                        return None
                    t = max(t, fin[d] + (hop if d.eng != e or d.stream is not None else 0.0))
            return t

        while remaining:
            best = None
            for e in self.ENGS:
                us = units[e]
                if not us:
                    continue
                win = window if (e in engines or e == "pool") else 1
                cand = None
                seen_dma = False
                for i in range(min(win, len(us))):
                    if e == "pool":
                        if us[i][0].stream is not None:
                            if seen_dma:
                                continue
                            seen_dma = True
                    rt = ready_time(us[i], e)
                    if rt is None:
                        continue
                    start = max(rt, free[e])
                    if LS_PRIO == "cp":
                        key = (start if start > free[e] + 1e-9 else free[e], -max(cpl[o] for o in us[i]))
                        if cand is None or key < cand[2]:
                            cand = (start, i, key)
                        continue
                    if cand is None or start < cand[0] - 1e-9:
                        cand = (start, i, None)
                    if rt <= free[e]:
                        break
                if cand is None:
                    continue
                if best is None or cand[0] < best[0]:
                    best = (cand[0], e, cand[1])
            assert best is not None, "list scheduler deadlock"
            start, e, i = best
            unit = units[e].pop(i)
            t = start
            for o in unit:
                if o.stream is not None:
                    fin[o] = t + 2.0 + o.cost
                    t += 0.1
                else:
                    t += o.cost
                    fin[o] = t
            free[e] = t
            out[e].extend(unit)
            remaining -= 1
        for e in self.ENGS:
            self.q[e] = out[e]
        self.est_time = max(free.values())

    def finalize(self):
        for e in self.ENGS:
            for o in self.q[e]:
                for d in o.deps:
                    if d.stream is None:
                        if d.eng == o.eng and (d.eng in SELF_ORDERED):
                            continue
                        d.signal = True
        for e in self.ENGS:
            c = 0
            for o in self.q[e]:
                if o.stream is None and o.signal:
                    c += 1
                    o.sigval = c
        for s, lst in self.stream_ops.items():
            for k, o in enumerate(lst):
                o.sigval = 16 * (k + 1)

    def emit(self, nc, block, sems):
        if LIST_SCHED:
            self.list_schedule(window=LS_WINDOW, hop=LS_HOP)
        self.finalize()
        sched = self

        def replay(ename, eng):
            waited = {}
            for o in sched.q[ename]:
                need = {}
                for d in o.deps:
                    if d.stream is None:
                        if d.eng == ename and (ename in SELF_ORDERED):
                            continue
                        key = d.eng
                    else:
                        key = d.stream
                    if d.sigval > need.get(key, 0):
                        need[key] = d.sigval
                for key, val in need.items():
                    if waited.get(key, 0) >= val:
                        continue
                    waited[key] = val
                    eng.wait_ge(sems[key], val)
                ins = o.fn(eng)
                if o.signal:
                    if o.stream is None:
                        ins.then_inc(sems[ename], 1)
                    else:
                        ins.then_inc(sems[o.stream], 16)

        @block.tensor
        def _(eng):
            replay("pe", eng)

        @block.scalar
        def _(eng):
            replay("act", eng)

        @block.vector
        def _(eng):
            replay("dve", eng)

        @block.gpsimd
        def _(eng):
            replay("pool", eng)

        @block.sync
        def _(eng):
            replay("sp", eng)


def _rel_bucket(d):
    n = np.maximum(d, 0).astype(np.int32)
    max_exact = 16
    nf = np.maximum(n, 1).astype(np.float32)
    large = max_exact + (np.log(nf / np.float32(max_exact)) / np.float32(math.log(128 / max_exact))
                         * np.float32(32 - max_exact)).astype(np.int32)
    large = np.minimum(large, 31)
    return np.where(n < max_exact, n, large)


def _static_consts():
    c = {}
    c["ident"] = np.eye(128, dtype=np.float32)
    d = np.arange(128)
    bk = _rel_bucket(d)
    oh = np.zeros((32, 128), np.float32)
    oh[bk, d] = 1.0
    c["onehot"] = oh
    m = np.arange(256)
    R = np.zeros((128, 256), np.float32)
    R[(128 - m) % 128, m] = 1.0
    c["rsel"] = R
    j = np.arange(128)[:, None]
    i = np.arange(128)[None, :]
    c["mcur"] = np.where(j <= i, 0.0, MASK).astype(np.float32)
    c["mprev"] = np.where(j > i, 0.0, MASK).astype(np.float32)
    w = np.array([2, 4, 8, 16])[:, None]
    pos = np.arange(16)[None, :]
    cnt = np.minimum(w, pos + 1).astype(np.float32)
    c["cnt"] = np.broadcast_to(cnt.reshape(1, 64), (128, 64)).copy()
    return c


def build_nc():
    nc = bass.Bass("TRN2", target_bir_lowering=False)

    def din(name, shape):
        return nc.dram_tensor(name, list(shape), F32, kind="ExternalInput").ap()

    def dout(name, shape):
        return nc.dram_tensor(name, list(shape), F32, kind="ExternalOutput").ap()

    xp = din("xp", [SEQ, D])
    xs = din("xs", [NS, D])
    kc_d = din("kc", [NB, 128, 128])
    vc_d = din("vc", [NB, 128, 128])
    spl_d = din("spl", [NB * 15, 512])
    meta_d = din("meta", [16, D])
    lnp_d = din("lnp", [6, D])
    rel_d = din("rel", [32, 8])
    win_d = din("win", [D, 1280])
    wpool_d = din("wpool", [4, 128, 128])
    pscale_d = din("pscale", [128, 4])
    esink_d = din("sinkp", [128, 4])
    wout_d = din("wout", [D, D])
    w1_d = din("w1", [D, 4096])
    w2_d = din("w2", [4096, D])
    ident_d = din("ident", [128, 128])
    onehot_d = din("onehot", [32, 128])
    rsel_d = din("rsel", [128, 256])
    mcur_d = din("mcur", [128, 128])
    mprev_d = din("mprev", [128, 128])
    cnt_d = din("cnt", [128, 64])

    y_p = dout("y_p", [SEQ, D])
    y_s = dout("y_s", [NS, D])
    nk_p = dout("nk_p", [128, 128])
    nv_p = dout("nv_p", [128, 128])
    np_p = dout("np_p", [15, 512])
    nk_s = dout("nk_s", [NB, 128, 128])
    nv_s = dout("nv_s", [NB, 128, 128])
    np_s = dout("np_s", [NB, 15, 512])

    import os as _os0
    KSKIP = set(_os0.environ.get("KSKIP", "").split(","))
    S = Sched()
    from contextlib import ExitStack
    es = ExitStack()

    ARENA_F32 = 53200
    arena = es.enter_context(nc.sbuf_tensor("arena", [128, ARENA_F32], F32))
    psum = es.enter_context(nc.psum_tensor("ps", [128, 8 * 512], F32))

    class Bump:
        def __init__(self, start=0):
            self.off = start

        def f32(self, n):
            a = arena[:, self.off:self.off + n]
            self.off += n
            assert self.off <= ARENA_F32, self.off
            return a

        def bf16(self, n):
            assert n % 2 == 0
            a = arena[:, self.off:self.off + n // 2].bitcast(BF16)
            self.off += n // 2
            assert self.off <= ARENA_F32, self.off
            return a

    def bank(b, n=512, off=0):
        return psum[:, b * 512 + off: b * 512 + off + n]

    BK = [Res("bank%d" % b) for b in range(8)]
    psb = psum.bitcast(BF16)

    P = Bump(0)
    S_main = P.f32(17 * 1024)
    h1T = P.bf16(8 * 2112)
    h1T3 = h1T.rearrange("p (k t) -> p k t", k=8)
    R_S = [Res("S%d" % i) for i in range(17)]
    R_h1T = [Res("h1T%d" % i) for i in range(17)]
    mhalf = P.f32(2)[:, 0:1]
    persist_end = P.off

    def Stile(i):
        return S_main[:, i * 1024:(i + 1) * 1024]

    A = Bump(persist_end)
    w_in = A.bf16(8 * 1280).rearrange("p (k f) -> p k f", k=8)
    w_out = A.bf16(8 * 1024).rearrange("p (k f) -> p k f", k=8)
    w_pool = A.bf16(4 * 128).rearrange("p (g f) -> p g f", g=4)
    G0 = A.f32(1024); B0 = A.f32(1024); G1 = A.f32(1024); B1 = A.f32(1024)
    ident_f = A.f32(128)
    ident_b = A.bf16(128)
    ones_b = A.bf16(64)
    Bcur = A.bf16(1024)
    Bprev = A.bf16(1024)
    pscale = A.f32(4)
    esink = A.f32(4)
    rec_b = A.f32(512)
    mcur = rec_b[:, 0:128]; mprev = rec_b[:, 128:256]
    cnt = A.f32(64)
    rcnt = A.f32(64)
    tbT = A.f32(8)
    rel_sb = A.f32(8)
    onehot = A.f32(128)
    rsel_b = A.bf16(256)
    tbT_b = A.bf16(8)
    S0 = A.f32(1024)
    h0T = A.bf16(8 * 128).rearrange("p (k t) -> p k t", k=8)
    kT = [A.bf16(128), A.bf16(128)]
    vtok = [A.bf16(128), A.bf16(128)]
    mixT = A.bf16(8 * 128).rearrange("p (k t) -> p k t", k=8)
    pTe = [[A.bf16(512), A.bf16(512)], [A.bf16(512), A.bf16(512)]]
    rec = A.f32(512)
    stg = A.f32(768)
    stat = A.f32(16)
    uT = A.f32(4 * 144).rearrange("p (g t) -> p g t", g=4)
    wsA = A.f32(4 * 144).rearrange("p (g t) -> p g t", g=4)
    wsB = A.f32(4 * 144).rearrange("p (g t) -> p g t", g=4)
    ppT = A.bf16(4 * 128).rearrange("p (g t) -> p g t", g=4)
    dtmp = A.f32(64)
    h0T_b = A.bf16(8 * 128).rearrange("p (k t) -> p k t", k=8)
    qT_b = A.bf16(4 * 128)
    qT = qT_b
    hbuf = A.bf16(1024)
    qzp = [[A.bf16(512), A.bf16(512)], [A.bf16(512), A.bf16(512)]]
    mixT_b = A.bf16(8 * 128).rearrange("p (k t) -> p k t", k=8)
    pTe_b = [[A.bf16(512), A.bf16(512)], [A.bf16(512), A.bf16(512)]]
    stat_b = A.f32(16)
    ppT_b = A.bf16(4 * 128).rearrange("p (g t) -> p g t", g=4)
    uT_b = A.f32(4 * 144).rearrange("p (g t) -> p g t", g=4)
    kT.append(A.bf16(128)); vtok.append(A.bf16(128))
    CTX = [dict(h0T=h0T, qT=qT, mixT=mixT, pTe=pTe, rec=rec, stat=stat, ppT=ppT, uT=uT, sfx=""),
           dict(h0T=h0T_b, qT=qT_b, mixT=mixT_b, pTe=pTe_b, rec=rec_b, stat=stat_b, ppT=ppT_b, uT=uT_b, sfx="_b")]
    CUR = dict(CTX[0])
    PX = [""]

    def use(p):
        CUR.clear(); CUR.update(CTX[p]); PX[0] = CTX[p]["sfx"]
    phase1_end = A.off

    Q = Bump(7 * 1024)
    kcb = Q.bf16(16 * 128).rearrange("p (b f) -> p b f", b=16)
    kcT = Q.bf16(16 * 128).rearrange("p (b f) -> p b f", b=16)
    vcb = Q.bf16(16 * 128).rearrange("p (b f) -> p b f", b=16)
    uext = Q.f32(4 * 16 * 19).rearrange("p (g b t) -> p g b t", g=4, b=16)
    xsA = Q.f32(4 * 16 * 19).rearrange("p (g b t) -> p g b t", g=4, b=16)
    xsB = Q.f32(4 * 16 * 19).rearrange("p (g b t) -> p g b t", g=4, b=16)
    spl = Q.f32(2 * 512).rearrange("p (h f) -> p h f", h=2)
    Bs = Q.bf16(512)
    Bn = Q.bf16(512)
    pTc = Q.bf16(512)
    pTn = Q.bf16(512)
    ppTs = Q.bf16(4 * 64).rearrange("p (g t) -> p g t", g=4)
    qz = [Q.bf16(4 * 64), Q.bf16(4 * 64)]
    assert Q.off <= 16 * 1024, Q.off

    R = {}

    def res(n):
        if n not in R:
            R[n] = Res(n)
        return R[n]

    def fsz(ap):
        n = 1
        for s_ in ap.shape[1:]:
            n *= s_
        return n

    def dma(eng, out, in_, stream, reads=(), writes=(), free_writes=()):
        return S.op(eng, lambda e: e.dma_start(out=out, in_=in_), reads=reads, writes=writes,
                    stream=stream, free_writes=free_writes, cost=fsz(out) * out.shape[0] * 4 / 3.0e5)

    def act(out, in_, func, reads, writes, **kw):
        return S.op("act", lambda e: e.activation(out=out, in_=in_, func=func, **kw), reads=reads, writes=writes,
                    cost=0.2 + fsz(out) / 1.0e3)

    def mm(out, lhsT, rhs, start, stop, reads, writes):
        return S.op("pe", lambda e: e.matmul(out, lhsT, rhs, start=start, stop=stop), reads=reads, writes=writes,
                    cost=0.012 + max(fsz(rhs), 64) / 2.4e3, gstart=bool(start), gstop=bool(stop))

    def tr(out, in_, ident, reads, writes):
        return S.op("pe", lambda e: e.transpose(out, in_, ident), reads=reads, writes=writes, cost=0.11)

    def tt(eng, out, in0, in1, op, reads, writes):
        return S.op(eng, lambda e: e.tensor_tensor(out=out, in0=in0, in1=in1, op=op), reads=reads, writes=writes,
                    cost=(0.08 + fsz(out) / 0.96e3) if eng == "dve" else (0.3 + fsz(out) / 0.5e3))

    def ts(eng, out, in0, s1, s2, op0, op1, reads, writes):
        c_ = 0.08 + fsz(out) / 0.96e3
        if op1 is None:
            return S.op(eng, lambda e: e.tensor_scalar(out, in0, s1, None, op0), reads=reads, writes=writes, cost=c_)
        return S.op(eng, lambda e: e.tensor_scalar(out, in0, s1, s2, op0, op1), reads=reads, writes=writes, cost=c_)

    def stt(eng, out, in0, scalar, in1, op0, op1, reads, writes):
        return S.op(eng, lambda e: e.scalar_tensor_tensor(out=out, in0=in0, scalar=scalar, in1=in1, op0=op0, op1=op1),
                    reads=reads, writes=writes, cost=(0.1 + fsz(out) / 0.8e3) if eng == "dve" else (0.3 + fsz(out) / 0.5e3))

    def cp(eng, out, in_, reads, writes):
        return S.op(eng, lambda e: e.tensor_copy(out=out, in_=in_), reads=reads, writes=writes, cost=0.08 + fsz(out) / 0.96e3)

    def mset(eng, ap, val, writes):
        return S.op(eng, lambda e: e.memset(ap, val), writes=writes, cost=0.06 + fsz(ap) / 1.9e3)

    r_par = res("params")
    r_p0 = res("p0"); r_p1 = res("p1")
    r_xs = R_S[16]
    r_S0 = res("S0")
    r_kcb = res("kcb"); r_vcb = res("vcb"); r_spl = res("spl")
    r_win = res("w_in"); r_wout = res("w_out"); r_wpool = res("w_pool")
    dma("sp", Stile(16)[0:NS, :], xs, "x16", writes=[r_xs])
    dma("sp", G0, lnp_d[0:1, :].partition_broadcast(128), "p0", free_writes=[r_p0])
    dma("sp", B0, lnp_d[1:2, :].partition_broadcast(128), "p0", free_writes=[r_p0])
    par_list = [
        (ident_f, ident_d), (onehot[0:32, :], onehot_d), (mcur, mcur_d), (mprev, mprev_d),
        (cnt, cnt_d), (rel_sb[0:32, :], rel_d), (pscale, pscale_d), (esink, esink_d),
    ]
    for o_, i_ in par_list:
        dma("sp", o_, i_, "params", free_writes=[r_par])
    win_v = win_d.rearrange("(k p) f -> p k f", p=128)
    wout_v = wout_d.rearrange("(k p) f -> p k f", p=128)
    for kk in range(0, 8, 2):
        dma("pool", w_in[:, kk:kk + 2, :], win_v[:, kk:kk + 2, :], "w_in", free_writes=[r_win])
    dma("pool", rsel_b, rsel_d, "rselb", writes=[res("rsel_b")])
    dma("pool", w_pool, wpool_d.rearrange("g c d -> c g d"), "w_pool", writes=[r_wpool])
    for b4 in range(0, NB, 2):
        dma("pool", kcb[:, b4:b4 + 2, :], kc_d[b4:b4 + 2].rearrange("b k f -> k b f"), "kc", free_writes=[r_kcb])
        dma("pool", vcb[:, b4:b4 + 2, :], vc_d[b4:b4 + 2].rearrange("b k f -> k b f"), "vc", free_writes=[r_vcb])
    for kk in range(0, 8, 2):
        dma("pool", w_out[:, kk:kk + 2, :], wout_v[:, kk:kk + 2, :], "w_out", free_writes=[r_wout])
    dma("sp", spl[0:120, :, :], spl_d.rearrange("(h r) f -> r h f", h=2), "spl", writes=[r_spl])
    mset("dve", S0[0:96, :], 0.0, writes=[r_S0])
    mset("dve", S0[96:128, :], 0.0, writes=[r_S0])
    dma("sp", S0[112:128, :], meta_d, "x0", writes=[r_S0])
    dma("sp", Stile(0), xp[0:128, :], "xt0", writes=[R_S[0]])
    dma("sp", G1, lnp_d[2:3, :].partition_broadcast(128), "p1", free_writes=[r_p1])
    dma("sp", B1, lnp_d[3:4, :].partition_broadcast(128), "p1", free_writes=[r_p1])

    r_c = res("consts")
    cp("dve", ident_b, ident_f, [r_par], [r_c])
    mset("dve", ones_b, 1.0, [res("ones")])
    for p_ in range(2):
        mset("dve", qzp[p_][0][64:128, :], 0.0, [res("qz%d" % p_)])
        mset("dve", qzp[p_][1][0:64, :], 0.0, [res("qz%d" % p_)])
    mset("dve", mhalf, -0.5, [res("mhalf")])
    S.op("dve", lambda e: e.reciprocal(rcnt, cnt), reads=[r_par], writes=[res("rcnt")])
    act(esink, esink, AF.Exp, [r_par], [r_par])
    mm(bank(0, 8), onehot[0:32, :], rel_sb[0:32, :], True, True, [r_par], [BK[0]])
    cp("dve", tbT, bank(0, 8), [BK[0]], [res("tbT")])
    cp("dve", tbT_b, tbT, [res("tbT")], [res("tbT_b")])
    for i in range(128):
        b_ = 2 + (i // 64)
        mm(bank(b_, 8, (i % 64) * 8), rsel_b[:, 128 - i:256 - i], tbT_b, True, True, [res("rsel_b"), res("tbT_b")], [BK[b_]])
    r_B = res("Btiles")
    for h in range(8):
        for hb in range(2):
            src = psum[:, (2 + hb) * 512:(3 + hb) * 512].rearrange("p (i h) -> p h i", h=8)[:, h, :]
            tt("dve", Bcur[:, h * 128 + hb * 64: h * 128 + hb * 64 + 64], src, mcur[:, hb * 64:(hb + 1) * 64], ALU.add,
               [BK[2 + hb], r_par], [r_B])
            tt("dve", Bprev[:, h * 128 + hb * 64: h * 128 + hb * 64 + 64], src, mprev[:, hb * 64:(hb + 1) * 64], ALU.add,
               [BK[2 + hb], r_par], [r_B])
    r_Bs = res("Bs")
    Bs5 = Bs.rearrange("p (k b g t) -> p k b g t", k=2, b=16, g=4)
    Bprev4 = Bprev.rearrange("p (k g i) -> p k g i", k=2, g=4)
    Bcur4 = Bcur.rearrange("p (k g i) -> p k g i", k=2, g=4)
    for b in range(NB):
        cp("dve", Bs5[:, :, b, :, :], Bprev4[:, :, :, 0:4], [r_B], [r_Bs])
    r_Bn0 = res("Bn_init"); r_Bn = res("Bn")
    mset("dve", Bn[0:64, :], MASK, [r_Bn0])
    mset("dve", Bn[64:128, :], 0.0, [r_Bn0])
    Bn5 = Bn.rearrange("p (k b g t) -> p k b g t", k=2, b=16, g=4)
    for b in range(NB if "bn" not in KSKIP else 0):
        for k_ in range(2):
            dma("sp", Bn5[4 * b:4 * b + 4, k_, b, :, :], Bcur4[0:4, k_, :, 0:4], "bn", reads=[r_B, r_Bn0], free_writes=[r_Bn])
    KXW = int(_os0.environ.get("KXW", "0"))
    for i in range(1, 7):
        dma("sp", Stile(i), xp[i * 128:(i + 1) * 128, :], "xt%d" % i, reads=[r_win] if (i == KXW) else (), writes=[R_S[i]])
    dma("sp", nk_s[:, 0:124, :], kc_d[:, 4:128, :], "d2d")
    dma("sp", nv_s[:, 0:124, :], vc_d[:, 4:128, :], "d2d")
    dma("sp", np_s[:, 0:11, :], spl_d.rearrange("(b j) f -> b j f", j=15)[:, 4:15, :], "d2d")


    import os as _os
    KSTOP = _os.environ.get("KSTOP", "")
    KLN = _os0.environ.get("KLN", "stt")
    KLN1 = _os0.environ.get("KLN1", "stt")

    def layernorm(Sap, rows, G, Bt, rS, engine_gb="pool", stat_ap=None, r_g=None, mode=None):
        r_st = res("stat" + PX[0]) if stat_ap is None else res("stat2_%d" % id(stat_ap))
        stat = stat_ap if stat_ap is not None else CUR["stat"]
        x = Sap[0:rows, :]
        for c2 in range(2):
            S.op("dve", lambda e, c2=c2: e.bn_stats(stat[0:rows, c2 * 6:(c2 + 1) * 6], Sap[0:rows, c2 * 512:(c2 + 1) * 512]),
                 reads=[rS], writes=[r_st])
        S.op("dve", lambda e: e.bn_aggr(stat[0:rows, 12:14], stat[0:rows, 0:12]), reads=[r_st], writes=[r_st])
        ts("dve", stat[0:rows, 15:16], stat[0:rows, 13:14], EPS, None, ALU.add, None, [r_st], [r_st])
        tt("pool", stat[0:rows, 14:15], stat[0:rows, 15:16], mhalf[0:rows, :], ALU.pow, [r_st, res("mhalf")], [r_st])
        if (mode or KLN) == "mix":
            stt("dve", x, x, stat[0:rows, 12:13], G[0:rows, :], ALU.subtract, ALU.mult, [rS, r_st, r_g or r_par], [rS])
            stt("pool", x, x, stat[0:rows, 14:15], Bt[0:rows, :], ALU.mult, ALU.add, [rS, r_st, r_g or r_par], [rS])
        elif (mode or KLN) == "stt":
            stt("dve", x, x, stat[0:rows, 12:13], G[0:rows, :], ALU.subtract, ALU.mult, [rS, r_st, r_g or r_par], [rS])
            stt("dve", x, x, stat[0:rows, 14:15], Bt[0:rows, :], ALU.mult, ALU.add, [rS, r_st, r_g or r_par], [rS])
        else:
            ts("dve", x, x, stat[0:rows, 12:13], stat[0:rows, 14:15], ALU.subtract, ALU.mult, [rS, r_st], [rS])
            tt(engine_gb, x, x, G[0:rows, :], ALU.mult, [rS, r_g or r_par], [rS])
            tt(engine_gb, x, x, Bt[0:rows, :], ALU.add, [rS, r_g or r_par], [rS])

    KTR = _os0.environ.get("KTR", "f32")

    def transpose_to(Sap, rows, rS, dst3, tok0, rdst):
        if KTR == "f32":
            for kc in range(8):
                b_ = kc // 4
                tr(bank(b_, rows, (kc % 4) * 128), Sap[0:rows, kc * 128:(kc + 1) * 128], ident_f[0:rows, 0:rows],
                   [rS, r_par], [BK[b_]])
            for b_ in range(2):
                src_ = bank(b_).rearrange("p (k t) -> p k t", k=4)[:, :, 0:rows]
                act(dst3[:, b_ * 4:(b_ + 1) * 4, tok0:tok0 + rows], src_, AF.Copy, [BK[b_]], [rdst])
            return
        r_hb = res("hb")
        if KTR == "bf16dve":
            cp("dve", hbuf[0:rows, :], Sap[0:rows, :], [rS], [r_hb])
        else:
            act(hbuf[0:rows, :], Sap[0:rows, :], AF.Copy, [rS], [r_hb])
        for kc in range(8):
            tr(psb[:, kc * 128:kc * 128 + rows], hbuf[0:rows, kc * 128:(kc + 1) * 128], ident_b[0:rows, 0:rows],
               [r_hb, r_c], [BK[0]])
        src_ = psb[:, 0:1024].rearrange("p (k t) -> p k t", k=8)[:, :, 0:rows]
        act(dst3[:, 0:8, tok0:tok0 + rows], src_, AF.Copy, [BK[0]], [rdst])

    KIPO = _os0.environ.get("KIPO", "ufirst")

    def in_proj(rows, r_h0T, r_uT, uT_dst, r_qT, kT_dst, r_kT, v_dst, r_v, tokmajor_extra, q_dst=None, skip_q=False):
        h0T = CUR["h0T"]; qT = CUR["qT"]
        for oc in ((4, 5, 6, 7, 8, 0, 1, 2, 3) if KIPO == "qfirst" else range(9)):
            if skip_q and 4 <= oc < 8:
                continue
            b_ = 2 if oc < 4 else (3 if oc < 8 else 4)
            off = (oc % 4) * 128 if oc < 8 else 0
            for kc in range(8):
                mm(bank(b_, rows, off), w_in[:, kc, oc * 128:(oc + 1) * 128], h0T[:, kc, 0:rows], kc == 0, kc == 7,
                   [r_win, r_h0T], [BK[b_]])
        for kc in range(8):
            mm(bank(4, 128, 128)[0:rows, :], h0T[:, kc, 0:rows], w_in[:, kc, 1152:1280], kc == 0, kc == 7,
               [r_win, r_h0T], [BK[4]])
        if tokmajor_extra:
            for kc in range(8):
                mm(bank(4, 128, 256)[0:rows, :], h0T[:, kc, 0:rows], w_in[:, kc, 1024:1152], kc == 0, kc == 7,
                   [r_win, r_h0T], [BK[4]])
            for kc in range(8):
                mm(bank(5)[0:rows, :], h0T[:, kc, 0:rows], w_in[:, kc, 0:512], kc == 0, kc == 7,
                   [r_win, r_h0T], [BK[5]])
        if KIPO != "qfirst":
            uT_dst(bank(2).rearrange("p (g t) -> p g t", g=4)[:, :, 0:rows])
        if skip_q:
            pass
        elif q_dst is not None:
            q_dst()
        else:
            act(qT.rearrange("p (g t) -> p g t", g=4)[:, :, 0:rows], bank(3).rearrange("p (g t) -> p g t", g=4)[:, :, 0:rows],
                AF.Copy, [BK[3]], [r_qT], scale=0.125)
        if KIPO == "qfirst":
            uT_dst(bank(2).rearrange("p (g t) -> p g t", g=4)[:, :, 0:rows])
        cp("dve", kT_dst[:, 0:rows], bank(4, rows, 0), [BK[4]], [r_kT])
        cp("dve", v_dst[0:rows, :], bank(4, 128, 128)[0:rows, :], [BK[4]], [r_v])
        if tokmajor_extra:
            r_stg = res("stg")
            cp("dve", stg[0:rows, 512:640], bank(4, 128, 256)[0:rows, :], [BK[4]], [r_stg])
            cp("dve", stg[0:rows, 640:768], bank(4, 128, 128)[0:rows, :], [BK[4]], [r_stg])
            act(stg[0:rows, 0:512], bank(5)[0:rows, :], AF.Copy, [BK[5]], [r_stg])

    def window_sums(u4, wa, wb, L, r_u, r_ws, eng="dve"):
        def sl(ap, g0, g1, a, b):
            return ap[:, g0:g1, ..., a:b] if False else ap[(slice(None), slice(g0, g1)) + (slice(None),) * (len(ap.shape) - 3) + (slice(a, b),)]
        tt(eng, sl(wa, 0, 4, 1, L), sl(u4, 0, 4, 1, L), sl(u4, 0, 4, 0, L - 1), ALU.add, [r_u], [r_ws])
        tt(eng, sl(wb, 1, 4, 3, L), sl(wa, 1, 4, 3, L), sl(wa, 1, 4, 1, L - 2), ALU.add, [r_ws], [r_ws])
        tt(eng, sl(wa, 2, 4, 7, L), sl(wb, 2, 4, 7, L), sl(wb, 2, 4, 3, L - 4), ALU.add, [r_ws], [r_ws])
        tt(eng, sl(wb, 3, 4, 15, L), sl(wa, 3, 4, 15, L), sl(wa, 3, 4, 7, L - 8), ALU.add, [r_ws], [r_ws])

    WIN = (2, 4, 8, 16)

    def pool_mm(ppT_ap, rows, r_pp, r_mix):
        mixT = CUR["mixT"]
        for g in range(4):
            mm(bank(5, rows, g * 128), w_pool[:, g, :], ppT_ap[:, g, 0:rows], True, True, [r_wpool, r_pp], [BK[5]])
        for g in range(4):
            act(mixT[:, g, 0:rows], bank(5, rows, g * 128), AF.Copy, [BK[5], r_par], [r_mix], scale=pscale[:, g:g + 1])

    def attn_finish(ncols, view, mix_out, r_mix):
        r_rec = res("rec" + PX[0])
        rec = CUR["rec"]
        if "actrec" in KSKIP:
            for g in range(4):
                ts("dve", view(rec[:, 0:ncols])[:, g], view(bank(5, ncols))[:, g], esink[:, g:g + 1], None, ALU.add, None,
                   [BK[5], r_par], [r_rec])
            S.op("dve", lambda e: e.reciprocal(rec[:, 0:ncols], rec[:, 0:ncols]), reads=[r_rec], writes=[r_rec])
        else:
            for g in range(4):
                act(view(rec[:, 0:ncols])[:, g], view(bank(5, ncols))[:, g], AF.Ln, [BK[5], r_par], [r_rec], bias=esink[:, g:g + 1])
            act(rec[:, 0:ncols], rec[:, 0:ncols], AF.Exp, [r_rec], [r_rec], scale=-1.0)
        tt("dve", mix_out, view(bank(4, ncols)), view(rec[:, 0:ncols]), ALU.mult, [BK[4], r_rec], [r_mix])

    def out_proj_ln1(Sap, rows, rS, r_mix, dst_tok0, r_dst, defer_T=False):
        mixT = CUR["mixT"]
        for half in range(2):
            for e_ in range(8):
                mm(bank(half)[0:rows, :], mixT[:, e_, 0:rows], w_out[:, e_, half * 512:(half + 1) * 512], e_ == 0, e_ == 7,
                   [r_wout, r_mix], [BK[half]])
        for half in range(2):
            stt("dve", Sap[0:rows, half * 512:(half + 1) * 512], Sap[0:rows, half * 512:(half + 1) * 512], ALPHA,
                bank(half)[0:rows, :], ALU.mult, ALU.add, [rS, BK[half]], [rS])
        layernorm(Sap, rows, G1, B1, rS, r_g=r_p1, mode=KLN1)
        if dst_tok0 is not None and not defer_T:
            transpose_to(Sap, rows, rS, h1T3, dst_tok0, r_dst)

    def finish():
        S.fence()
        S.op("sp", lambda e: e.nop())
        names = list(Sched.ENGS) + list(S.stream_ops.keys())
        sems = {}
        for nm in names:
            sems[nm] = es.enter_context(nc.semaphore("s_" + nm))
        block = es.enter_context(nc.Block())
        S.emit(nc, block, sems)
        es.close()
        return nc
    if KSTOP == "A":
        return finish()
    Ss = Stile(16)
    rSs = R_S[16]
    SSL = 2
    r_kT = [res("kT0"), res("kT1"), res("kT2")]; r_v = [res("v0"), res("v1"), res("v2")]
    RP = [dict(h0T=res("h0T"), qT=res("qT"), mix=res("mixT"), pTe=res("pTe"), pp=res("ppT"), uT=res("uT")),
          dict(h0T=res("h0T_b"), qT=res("qT_b"), mix=res("mixT_b"), pTe=res("pTe_b"), pp=res("ppT_b"), uT=res("uT_b"))]
    mset("dve", vtok[SSL][64:128, :], 0.0, [r_v[SSL]])
    mset("dve", pTn[64:128, :], 0.0, [res("pTs")])

    def sample_gen():
        C = CTX[1]
        rp = RP[1]
        r_uext = res("uext")
        qTs = C["qT"]
        qT4 = qTs.rearrange("p (g t) -> p g t", g=4)
        mixTs = C["mixT"]
        use(1)
        layernorm(Ss, NS, G0, B0, rSs, r_g=r_p0)
        yield
        use(1)
        transpose_to(Ss, NS, rSs, C["h0T"], 0, rp["h0T"])
        for hh in range(2):
            for g in range(4):
                tr(bank(6 + hh, 120, g * 128), spl[0:120, hh, g * 128:(g + 1) * 128], ident_f[0:120, 0:120], [r_spl, r_par], [BK[6 + hh]])
            for g in range(4):
                cp("dve", uext[:, g, hh * 8:(hh + 1) * 8, 0:15], bank(6 + hh, 120, g * 128).rearrange("p (b j) -> p b j", j=15),
                   [BK[6 + hh]], [r_uext])
        r_kcT = res("kcT")
        for b in range(NB):
            b_ = 6 + b // 8
            o_ = psb[:, b_ * 1024 + (b % 8) * 128: b_ * 1024 + (b % 8) * 128 + 128]
            tr(o_, kcb[:, b, :], ident_b, [r_kcb, r_c], [BK[b_]])
        for hh in range(2):
            cp("dve", kcT[:, hh * 8:(hh + 1) * 8, :], psb[:, (6 + hh) * 1024:(7 + hh) * 1024].rearrange("p (b k) -> p b k", b=8),
               [BK[6 + hh]], [r_kcT])
        yield
        use(1)

        def uT_dst_sample(src_):
            for g in range(4):
                act(uext[:, g, :, 15:19], src_[:, g, :].rearrange("p (b t) -> p b t", t=4), AF.Copy, [BK[2]], [r_uext])
        in_proj(NS, rp["h0T"], r_uext, uT_dst_sample, rp["qT"], kT[SSL], r_kT[SSL], vtok[SSL], r_v[SSL], True)
        r_stg = res("stg")
        dma("sp", np_s[:, 11:15, :], stg[0:NS, 0:512], "outs", reads=[r_stg])
        dma("sp", nk_s[:, 124:128, :], stg[0:NS, 512:640], "outs", reads=[r_stg])
        dma("sp", nv_s[:, 124:128, :], stg[0:NS, 640:768], "outs", reads=[r_stg])
        yield
        use(1)
        r_xs_ws = res("xs_ws")
        window_sums(uext, xsA, xsB, 19, r_uext, r_xs_ws)
        r_pps = res("ppTs")
        fin = [xsA, xsB, xsA, xsB]
        for g in range(4):
            stt("dve", ppTs[:, g, :].rearrange("p (b t) -> p b t", t=4), fin[g][:, g, :, 15:19], 1.0 / WIN[g],
                uext[:, g, :, 15:19], ALU.mult, ALU.subtract, [r_xs_ws, r_uext], [r_pps])
        pool_mm(ppTs, NS, r_pps, rp["mix"])
        r_qz = res("qz")
        for kvh in range(2):
            rows_ = slice(kvh * 64, kvh * 64 + 64)
            mset("dve", qz[kvh], 0.0, [r_qz])
            cp("dve", qz[kvh][rows_, :].rearrange("p (g t) -> p g t", g=4), qT4[rows_, :, 0:NS], [rp["qT"]], [r_qz])
        for kvh in range(2):
            rows_ = slice(kvh * 64, kvh * 64 + 64)
            bc = 6 + kvh
            bn_ = 2 + kvh
            mm(bank(bc, 256), ident_b, Bs[:, kvh * 256:(kvh + 1) * 256], True, False, [r_c, r_Bs], [BK[bc]])
            for b in range(NB):
                mm(bank(bc, 16, b * 16), kcT[rows_, b, :], qT4[rows_, :, 4 * b:4 * b + 4], False, b == NB - 1,
                   [r_kcT, rp["qT"]], [BK[bc]])
            mm(bank(bn_, 256)[0:NS, :], ident_b[:, 0:NS], Bn[:, kvh * 256:(kvh + 1) * 256], True, False, [r_c, r_Bn, r_Bn0], [BK[bn_]])
            qzb = qz[kvh].rearrange("p (g b t) -> p b g t", g=4, t=4)
            mm(bank(bn_, 256)[0:NS, :], kT[SSL][:, 0:NS], qzb, False, True, [r_kT[SSL], r_qz], [BK[bn_]])
        r_pTs = res("pTs")
        for kvh in range(2):
            act(pTc[:, kvh * 256:(kvh + 1) * 256], bank(6 + kvh, 256), AF.Exp, [BK[6 + kvh]], [r_pTs])
            act(pTn[0:NS, kvh * 256:(kvh + 1) * 256], bank(2 + kvh, 256)[0:NS, :], AF.Exp, [BK[2 + kvh]], [r_pTs])
        yield
        use(1)
        for kvh in range(2):
            rows_ = slice(kvh * 64, kvh * 64 + 64)
            mm(bank(4, 256)[rows_, :], vtok[SSL][:, kvh * 64:(kvh + 1) * 64], pTn[:, kvh * 256:(kvh + 1) * 256], True, False,
               [r_v[SSL], r_pTs], [BK[4]])
            for b in range(NB):
                mm(bank(4, 16, b * 16)[rows_, :], vcb[:, b, kvh * 64:(kvh + 1) * 64], pTc[:, kvh * 256 + b * 16:kvh * 256 + b * 16 + 16],
                   False, b == NB - 1, [r_vcb, r_pTs], [BK[4]])
            mm(bank(5, 256)[rows_, :], ones_b[:, 0:64], pTc[:, kvh * 256:(kvh + 1) * 256], True, False, [res("ones"), r_pTs], [BK[5]])
            mm(bank(5, 256)[rows_, :], ones_b[:, 0:64], pTn[:, kvh * 256:(kvh + 1) * 256], False, True,
               [res("ones"), r_pTs], [BK[5]])
        attn_finish(256, lambda a: a.rearrange("p (b g t) -> p g b t", b=16, g=4),
                    mixTs[:, 4:8, 0:NS].rearrange("p g (b t) -> p g b t", t=4), rp["mix"])
        yield
        use(1)
        out_proj_ln1(Ss, NS, rSs, rp["mix"], 2048, R_h1T[16], defer_T=True)
        yield
        use(1)
        transpose_to(Ss, NS, rSs, h1T3, 2048, R_h1T[16])

    def after_sample():
        S.fence(only=("sp",))
        for i in range(7, 16):
            dma("sp", Stile(i), xp[i * 128:(i + 1) * 128, :], "xt%d" % i, writes=[R_S[i]])

    r_ws = res("ws")
    mset("dve", CTX[0]["uT"][:, :, 0:16], 0.0, [RP[0]["uT"]])

    KPOOLENG = _os.environ.get("KPOOLENG", "dve")

    def tile_gen(t):
        p = t % 2
        rp = RP[p]
        Sap = S0 if t == 0 else Stile(t - 1)
        rS = r_S0 if t == 0 else R_S[t - 1]
        cur = t % 3
        prv = (t - 1) % 3
        last = (t == NTILE)
        C = CTX[p]
        uT_ = C["uT"]; ppT_ = C["ppT"]; pTe_ = C["pTe"]; mixT_ = C["mixT"]
        qT4_ = C["qT"].rearrange("p (g t) -> p g t", g=4)
        use(p)
        layernorm(Sap, 128, G0, B0, rS, r_g=r_p0)
        yield
        use(p)
        transpose_to(Sap, 128, rS, C["h0T"], 0, rp["h0T"])
        yield
        use(p)

        def uT_dst_p(src_):
            act(uT_[:, :, 16:144], src_, AF.Copy, [BK[2]], [rp["uT"]])
        r_qz_ = res("qz%d" % p)

        def q_dst_p():
            act(qzp[p][0][0:64, :], bank(3)[0:64, :], AF.Copy, [BK[3]], [r_qz_], scale=0.125)
            act(qzp[p][1][64:128, :], bank(3)[64:128, :], AF.Copy, [BK[3]], [r_qz_], scale=0.125)
        in_proj(128, rp["h0T"], rp["uT"], uT_dst_p, rp["qT"], kT[cur], r_kT[cur], vtok[cur], r_v[cur], last, q_dst=q_dst_p, skip_q=(t == 0))
        if last:
            r_stg = res("stg")
            dma("sp", np_p, stg[113:128, 0:512], "outp", reads=[r_stg])
            dma("sp", nk_p, stg[:, 512:640], "outp", reads=[r_stg])
            dma("sp", nv_p, stg[:, 640:768], "outp", reads=[r_stg])
        if t == 0:
            mset("dve", uT_[:, :, 16:128], 0.0, [rp["uT"]])
        yield
        use(p)
        if t == 0:
            cp("dve", CTX[1 - p]["uT"][:, :, 0:16], uT_[:, :, 128:144], [rp["uT"]], [RP[1 - p]["uT"]])
            return
        kbs = [1] if t == 0 else [0, 1]
        for kvh in range(2):
            rows_ = slice(kvh * 64, kvh * 64 + 64)
            for kb in kbs:
                b_ = (2 if kvh == 0 else 6) + kb
                ksl = prv if kb == 0 else cur
                mm(bank(b_), kT[ksl], qzp[p][kvh], True, False, [r_kT[ksl], r_qz_], [BK[b_]])
                Bt_ = Bprev if kb == 0 else Bcur
                mm(bank(b_), ident_b, Bt_[:, kvh * 512:(kvh + 1) * 512], False, True, [r_c, r_B], [BK[b_]])
        for kvh in range(2):
            for kb in kbs:
                b_ = (2 if kvh == 0 else 6) + kb
                act(pTe_[kvh][kb], bank(b_), AF.Exp, [BK[b_]], [rp["pTe"]])
                if (t == 0 and kb == 1) or (t == 1 and kb == 0):
                    mset("dve", pTe_[kvh][kb][0:112, :], 0.0, [rp["pTe"]])
        PE_ = KPOOLENG
        window_sums(uT_, wsA, wsB, 144, rp["uT"], r_ws, eng=PE_)
        fin = [wsA, wsB, wsA, wsB]
        for g in range(4):
            stt("dve", ppT_[:, g, :], fin[g][:, g, 16:144], 1.0 / WIN[g], uT_[:, g, 16:144], ALU.mult, ALU.subtract,
                [r_ws, rp["uT"]], [rp["pp"]])
        if t == 0:
            r_dt = res("dtmp")
            cnt3 = rcnt.rearrange("p (g t) -> p g t", g=4)
            dt3 = dtmp.rearrange("p (g t) -> p g t", g=4)
            for g in range(4):
                tt("dve", dt3[:, g, :], fin[g][:, g, 128:144], cnt3[:, g, :], ALU.mult, [r_ws, res("rcnt")], [r_dt])
                tt("dve", ppT_[:, g, 112:128], dt3[:, g, :], uT_[:, g, 128:144], ALU.subtract, [r_dt, rp["uT"]], [rp["pp"]])
        if not last:
            cp(KPOOLENG, CTX[1 - p]["uT"][:, :, 0:16], uT_[:, :, 128:144], [rp["uT"]], [RP[1 - p]["uT"]])
        pool_mm(ppT_, 128, rp["pp"], rp["mix"])
        yield
        use(p)
        for kvh in range(2):
            rows_ = slice(kvh * 64, kvh * 64 + 64)
            for n_, kb in enumerate(kbs):
                vsl = prv if kb == 0 else cur
                mm(bank(4)[rows_, :], vtok[vsl][:, kvh * 64:(kvh + 1) * 64], pTe_[kvh][kb], n_ == 0, n_ == len(kbs) - 1,
                   [r_v[vsl], rp["pTe"]], [BK[4]])
            for n_, kb in enumerate(kbs):
                mm(bank(5)[rows_, :], ones_b[:, 0:64], pTe_[kvh][kb], n_ == 0, n_ == len(kbs) - 1,
                   [res("ones"), rp["pTe"]], [BK[5]])
        attn_finish(512, lambda a: a.rearrange("p (g t) -> p g t", g=4), mixT_[:, 4:8, :], rp["mix"])
        yield
        use(p)
        out_proj_ln1(Sap, 128, rS, rp["mix"], None if t == 0 else (t - 1) * 128, None if t == 0 else R_h1T[t - 1], defer_T=True)
        yield
        use(p)
        if t > 0:
            transpose_to(Sap, 128, rS, h1T3, (t - 1) * 128, R_h1T[t - 1])

    Z = Bump(persist_end)
    _wb10 = Z.bf16(8 * 1024).rearrange("p (k f) -> p k f", k=8)
    Z.f32(1024)
    _wb20 = Z.bf16(8 * 1024).rearrange("p (k f) -> p k f", k=8)
    _wb11 = Z.bf16(8 * 1024).rearrange("p (k f) -> p k f", k=8)
    _wb21 = Z.bf16(8 * 1024).rearrange("p (k f) -> p k f", k=8)
    Wb1 = [_wb10, _wb11]
    Wb2 = [_wb20, _wb21]
    aT = [Z.bf16(8 * 512).rearrange("p (k t) -> p k t", k=8) for _ in range(2)]
    G2 = Z.f32(1024); B2t = Z.f32(1024)
    rtmp = [Z.f32(512), Z.f32(512)]
    stat2s = [Z.f32(16), Z.f32(16)]
    r_w1 = [res("wb1_0"), res("wb1_1")]; r_w2 = [res("wb2_0"), res("wb2_1")]
    r_aT = [res("aT0"), res("aT1")]; r_rt = [res("rt0"), res("rt1")]
    r_g2 = res("g2")
    w1_v = w1_d.rearrange("(k p) f -> p k f", p=128)
    w2_v = w2_d.rearrange("(k p) f -> p k f", p=128)


    def prefetch_w1q0():
        for kk in range(0, 8, 4):
            dma("pool", Wb1[0][:, kk:kk + 4, :], w1_v[:, kk:kk + 4, 0:1024], "w1q0", writes=[r_win, r_w1[0]] if kk == 0 else (),
                free_writes=() if kk == 0 else [r_w1[0]])

    def prefetch_w2q0():
        for kk in range(0, 8, 4):
            dma("pool", Wb2[0][:, kk:kk + 4, :], w2_v[:, kk:kk + 4, :], "w2q0", writes=[r_wout, r_w2[0]] if kk == 0 else (),
                free_writes=() if kk == 0 else [r_w2[0]])

    KPRE = _os.environ.get("KPRE", "1") == "1" and _os.environ.get("KDEBUG") != "1"
    NSTAGE = 7
    KORD = _os.environ.get("KORD", "old")
    SKEW = int(_os.environ.get("KSKEW", "1"))
    gens = [sample_gen()] + [tile_gen(t_) for t_ in range(NTILE + 1)]
    sample_done = False
    KS0 = int(_os.environ.get("KS0", "0"))
    sched = {}
    for j_ in range(NTILE + 2):
        for s_ in range(NSTAGE):
            st_ = SKEW * j_ + s_ if s_ >= 1 else max(0, SKEW * j_ - KS0)
            sched.setdefault(st_, []).append((j_, s_))
    for step in sorted(sched):
        for j_, s_ in sorted(sched[step]):
            try:
                next(gens[j_])
            except StopIteration:
                pass
            if KPRE and j_ == NTILE + 1 and s_ == 5:
                prefetch_w2q0()
            if KPRE and j_ == NTILE + 1 and s_ == 2:
                prefetch_w1q0()
            if j_ == 0 and s_ == NSTAGE - 1 and not sample_done:
                sample_done = True
                after_sample()
    use(0)

    if KSTOP == "C":
        return finish()
    DBG1 = _os.environ.get("KDEBUG") == "1"
    def load_quarter(q, after=()):
        sl = q % 2
        after = list(after)
        for kk in (range(0, 8, 4) if not (KPRE and q == 0) else ()):
            dma("pool", Wb1[sl][:, kk:kk + 4, :], w1_v[:, kk:kk + 4, q * 1024:(q + 1) * 1024], "w1q%d" % sl, reads=after,
                writes=[r_w1[sl]] if kk == 0 else (), free_writes=() if kk == 0 else [r_w1[sl]])
        for kk in (range(0, 8, 4) if not (KPRE and q == 0) else ()):
            dma("pool", Wb2[sl][:, kk:kk + 4, :], w2_v[:, q * 8 + kk:q * 8 + kk + 4, :], "w2q%d" % sl, writes=[r_w2[sl]] if kk == 0 else (),
                free_writes=() if kk == 0 else [r_w2[sl]])

    pre_ops = [o for o in S.stream_ops.get("w1q0", [])] + [o for o in S.stream_ops.get("w2q0", [])]
    S.fence()
    if KPRE:
        for e_ in S.ENGS:
            S.fence_deps[e_] = {d for d in S.fence_deps[e_] if d not in pre_ops}
    dma("sp", G2, lnp_d[4:5, :].partition_broadcast(128), "g2", free_writes=[r_g2])
    dma("sp", B2t, lnp_d[5:6, :].partition_broadcast(128), "g2", free_writes=[r_g2])
    macros = [([(4 * m + i, 128) for i in range(4)], 4 * m * 128) for m in range(4)] + [([(16, NS)], 2048)]
    if not DBG1:
        load_quarter(0)
    nmac = 0
    KSQ = _os.environ.get("KSQ", "act")
    for q in range(0 if DBG1 else 4):
        sl = q % 2
        for mi_, (tiles, tok0) in enumerate(macros):
            if mi_ == 1 and q + 1 < 4:
                load_quarter(q + 1, after=[r_aT[(nmac - 1) % 2]])
            n = sum(r for _, r in tiles)
            a_sl = nmac % 2
            nmac += 1
            rh = [R_h1T[i] for i, _ in tiles]
            for fc in range(8):
                b_ = fc % 4
                for kc in range(8):
                    mm(bank(b_, n), Wb1[sl][:, kc, fc * 128:(fc + 1) * 128], h1T3[:, kc, tok0:tok0 + n], kc == 0, kc == 7,
                       [r_w1[sl]] + rh, [BK[b_]])
                rt = fc % 2
                act(rtmp[rt][:, 0:n], bank(b_, n), AF.Relu, [BK[b_]], [r_rt[rt]])
                if KSQ == "act":
                    act(aT[a_sl][:, fc, 0:n], rtmp[rt][:, 0:n], AF.Square, [r_rt[rt]], [r_aT[a_sl]])
                elif KSQ == "pool":
                    tt("pool", aT[a_sl][:, fc, 0:n], rtmp[rt][:, 0:n], rtmp[rt][:, 0:n], ALU.mult, [r_rt[rt]], [r_aT[a_sl]])
                else:
                    tt("dve", aT[a_sl][:, fc, 0:n], rtmp[rt][:, 0:n], rtmp[rt][:, 0:n], ALU.mult, [r_rt[rt]], [r_aT[a_sl]])
            for si, (idx, rows) in enumerate(tiles):
                Sap = Stile(idx)
                for half in range(2):
                    b_ = 4 + (si % 2) * 2 + half
                    for fc in range(8):
                        mm(bank(b_)[0:rows, :], aT[a_sl][:, fc, si * 128:si * 128 + rows], Wb2[sl][:, fc, half * 512:(half + 1) * 512],
                           fc == 0, fc == 7, [r_aT[a_sl], r_w2[sl]], [BK[b_]])
                for half in range(2):
                    b_ = 4 + (si % 2) * 2 + half
                    dst = Sap[0:rows, half * 512:(half + 1) * 512]
                    if q == 0:
                        stt("dve", dst, dst, ALPHA, bank(b_)[0:rows, :], ALU.mult, ALU.add, [R_S[idx], BK[b_]], [R_S[idx]])
                    else:
                        tt("dve", dst, dst, bank(b_)[0:rows, :], ALU.add, [R_S[idx], BK[b_]], [R_S[idx]])
                if q == 3:
                    layernorm(Sap, rows, G2, B2t, R_S[idx], engine_gb="pool", stat_ap=stat2s[idx % 2], r_g=r_g2,
                              mode=("stt" if idx % 2 == 0 else "pool"))
                    if idx < 16:
                        dma("sp", y_p[idx * 128:(idx + 1) * 128, :], Sap, "outy", reads=[R_S[idx]])
                    else:
                        dma("sp", y_s, Sap[0:NS, :], "outy", reads=[R_S[idx]])
    S.fence()
    S.op("sp", lambda e: e.nop())

    names = list(Sched.ENGS) + list(S.stream_ops.keys())
    sems = {}
    for nm in names:
        sems[nm] = es.enter_context(nc.semaphore("s_" + nm))
    block = es.enter_context(nc.Block())
    S.emit(nc, block, sems)
    es.close()
    if DBG1:
        print("DBG offsets", dict(mixT=None))
        nc._dbg = dict(phase1_start=persist_end)
    return nc


_NC_CACHE = {}


def _get_nc():
    if "nc" not in _NC_CACHE:
        _NC_CACHE["nc"] = build_nc()
    return _NC_CACHE["nc"]


def kernel(x_prompt, x_sample, cache_win_k, cache_win_v, state_pool, meta_tokens, ln_emb_g, ln_emb_b,
           rel_table, w_in, w_pool, pool_scale, sinks, w_out, ln1_g, ln1_b, w_mlp_in, w_mlp_out, ln2_g, ln2_b):
    f = lambda a: np.ascontiguousarray(np.asarray(a, dtype=np.float32))
    x_prompt = f(x_prompt); x_sample = f(x_sample)
    ck = f(cache_win_k)[0].reshape(128, 128, 128)
    cv = f(cache_win_v)[0].reshape(128, 128, 128)
    sp = f(state_pool)[0]
    w_in0 = f(w_in)[0]
    qcols = []
    for c in range(4):
        qcols += list(range(512 + c * 64, 512 + c * 64 + 64)) + list(range(512 + (c + 4) * 64, 512 + (c + 4) * 64 + 64))
    cols = list(range(512)) + qcols + list(range(1024, 1280))
    win_p = np.ascontiguousarray(w_in0[:, cols])
    w_out0 = f(w_out)[0]
    rows = list(range(512)) + [r + 0 for r in qcols]
    wout_p = np.ascontiguousarray(w_out0[rows, :])
    lnp = np.stack([f(ln_emb_g), f(ln_emb_b), f(ln1_g)[0], f(ln1_b)[0], f(ln2_g)[0], f(ln2_b)[0]], axis=0)
    pscale = np.ascontiguousarray(f(pool_scale)[0].reshape(4, 128).T)
    sk = f(sinks)[0]
    sinkp = np.zeros((128, 4), np.float32)
    for g in range(4):
        sinkp[0:64, g] = sk[g]
        sinkp[64:128, g] = sk[4 + g]
    consts = _static_consts()
    common = dict(meta=f(meta_tokens), lnp=lnp, rel=f(rel_table), win=win_p, wpool=f(w_pool)[0], pscale=pscale,
                  sinkp=sinkp, wout=wout_p, w1=f(w_mlp_in)[0], w2=f(w_mlp_out)[0], **consts)
    in_maps = []
    for c in range(8):
        m = dict(common)
        m["xp"] = x_prompt[c]
        m["xs"] = x_sample[16 * c:16 * c + 16].reshape(NS, D)
        m["kc"] = ck[16 * c:16 * c + 16]
        m["vc"] = cv[16 * c:16 * c + 16]
        m["spl"] = sp[16 * c:16 * c + 16].reshape(NB * 15, 512)
        in_maps.append(m)
    nc = _get_nc()
    res_ = run_bass_kernel_spmd(nc, in_maps, core_ids=list(range(8)))
    rs = res_.results
    y_prompt = np.stack([rs[c]["y_p"] for c in range(8)], axis=0)
    y_sample = np.concatenate([rs[c]["y_s"].reshape(NB, TS, D) for c in range(8)], axis=0)
    nk_p = np.stack([rs[c]["nk_p"].reshape(128, 2, 64) for c in range(8)], axis=0)[None]
    nv_p = np.stack([rs[c]["nv_p"].reshape(128, 2, 64) for c in range(8)], axis=0)[None]
    np_p = np.stack([rs[c]["np_p"] for c in range(8)], axis=0)[None]
    nk_s = np.concatenate([rs[c]["nk_s"].reshape(NB, 128, 2, 64) for c in range(8)], axis=0)[None]
    nv_s = np.concatenate([rs[c]["nv_s"].reshape(NB, 128, 2, 64) for c in range(8)], axis=0)[None]
    np_s = np.concatenate([rs[c]["np_s"] for c in range(8)], axis=0)[None]
    return (y_prompt.astype(np.float32), y_sample.astype(np.float32), nk_p.astype(np.float32), nv_p.astype(np.float32),
            np_p.astype(np.float32), nk_s.astype(np.float32), nv_s.astype(np.float32), np_s.astype(np.float32))
```

```python
import math
import numpy as np
import concourse.bass as bass
import concourse.mybir as mybir
from concourse.bass_utils import run_bass_kernel_spmd

F32 = mybir.dt.float32
BF16 = mybir.dt.bfloat16
AF = mybir.ActivationFunctionType
ALU = mybir.AluOpType

D = 1024
NTILE = 16
SEQ = 2048
NS = 64
NB = 16
TS = 4
ALPHA = float(2.0 ** 0.25)
EPS = 1e-5
MASK = -30000.0
import os as _osx
LIST_SCHED = _osx.environ.get("KLIST", "1") == "1"
LS_WINDOW = int(_osx.environ.get("KLWIN", "100"))
LS_PRIO = _osx.environ.get("KLPRIO", "order")
LS_HOP = float(_osx.environ.get("KLHOP", "0.6"))
SELF_ORDERED = ("pe",)


class Res:
    __slots__ = ("name", "w", "readers")

    def __init__(self, name):
        self.name = name
        self.w = None
        self.readers = []


class Op:
    __slots__ = ("eng", "fn", "deps", "stream", "signal", "sigval", "gidx", "cost", "gstart", "gstop")

    def __init__(self, eng, fn, deps, stream):
        self.eng = eng
        self.fn = fn
        self.deps = deps
        self.stream = stream
        self.signal = stream is not None
        self.sigval = None
        self.cost = 0.3
        self.gstart = True
        self.gstop = True


class Sched:
    ENGS = ("pe", "act", "dve", "pool", "sp")

    def __init__(self):
        self.q = {e: [] for e in self.ENGS}
        self.stream_ops = {}
        self.fence_deps = {e: set() for e in self.ENGS}
        self.n = 0

    def op(self, eng, fn, reads=(), writes=(), stream=None, free_writes=(), cost=None, gstart=True, gstop=True):
        deps = set()
        for r in reads:
            if r.w is not None:
                deps.add(r.w)
        for w in writes:
            if w.w is not None:
                deps.add(w.w)
            deps.update(w.readers)
        if self.fence_deps[eng]:
            deps.update(self.fence_deps[eng])
            self.fence_deps[eng] = set()
        o = Op(eng, fn, deps, stream)
        if cost is not None:
            o.cost = cost
        o.gstart = gstart
        o.gstop = gstop
        o.gidx = self.n
        self.n += 1
        self.q[eng].append(o)
        if stream is not None:
            self.stream_ops.setdefault(stream, []).append(o)
        for r in reads:
            r.readers.append(o)
        for w in writes:
            w.w = o
            w.readers = []
        for w in free_writes:
            w.w = o
            w.readers = []
        return o

    def fence(self, only=None):
        last = set()
        for e in self.ENGS:
            for o in reversed(self.q[e]):
                if o.stream is None:
                    last.add(o)
                    break
        for s, lst in self.stream_ops.items():
            last.add(lst[-1])
        for e in (only or self.ENGS):
            self.fence_deps[e] = set(last)

    def list_schedule(self, engines=("pe", "act", "dve"), window=48, hop=0.15):
        units = {}
        for e in self.ENGS:
            us = []
            cur = None
            for o in self.q[e]:
                if e == "pe":
                    if cur is None:
                        cur = [o]
                    else:
                        cur.append(o)
                    if o.gstop:
                        us.append(cur)
                        cur = None
                else:
                    us.append([o])
            if cur:
                us.append(cur)
            units[e] = us
        succ = {}
        allops = [o for e in self.ENGS for o in self.q[e]]
        for o in allops:
            for d in o.deps:
                succ.setdefault(d, []).append(o)
        cpl = {}
        for o in sorted(allops, key=lambda o: -o.gidx):
            m = 0.0
            for s_ in succ.get(o, ()):
                m = max(m, cpl[s_] + (hop if s_.eng != o.eng else 0.0))
            cpl[o] = o.cost + m
        fin = {}
        free = {e: 0.0 for e in self.ENGS}
        out = {e: [] for e in self.ENGS}
        remaining = sum(len(u) for u in units.values())

        def ready_time(unit, e):
            t = 0.0
            inside = set(unit)
            for o in unit:
                for d in o.deps:
                    if d in inside:
                        continue
                    if d not in fin:
                        return None
                    t = max(t, fin[d] + (hop if d.eng != e or d.stream is not None else 0.0))
            return t

        while remaining:
            best = None
            for e in self.ENGS:
                us = units[e]
                if not us:
                    continue
                win = window if (e in engines or e == "pool") else 1
                cand = None
                seen_dma = False
                for i in range(min(win, len(us))):
                    if e == "pool":
                        if us[i][0].stream is not None:
                            if seen_dma:
                                continue
                            seen_dma = True
                    rt = ready_time(us[i], e)
                    if rt is None:
                        continue
                    start = max(rt, free[e])
                    if LS_PRIO == "cp":
                        key = (start if start > free[e] + 1e-9 else free[e], -max(cpl[o] for o in us[i]))
                        if cand is None or key < cand[2]:
                            cand = (start, i, key)
                        continue
                    if cand is None or start < cand[0] - 1e-9:
                        cand = (start, i, None)
                    if rt <= free[e]:
                        break
                if cand is None:
                    continue
                if best is None or cand[0] < best[0]:
                    best = (cand[0], e, cand[1])
            assert best is not None, "list scheduler deadlock"
            start, e, i = best
            unit = units[e].pop(i)
            t = start
            for o in unit:
                if o.stream is not None:
                    fin[o] = t + 2.0 + o.cost
                    t += 0.1
                else:
                    t += o.cost
                    fin[o] = t
            free[e] = t
            out[e].extend(unit)
            remaining -= 1
        for e in self.ENGS:
            self.q[e] = out[e]
        self.est_time = max(free.values())

    def finalize(self):
        for e in self.ENGS:
            for o in self.q[e]:
                for d in o.deps:
                    if d.stream is None:
                        if d.eng == o.eng and (d.eng in SELF_ORDERED):
                            continue
                        d.signal = True
        for e in self.ENGS:
            c = 0
            for o in self.q[e]:
                if o.stream is None and o.signal:
                    c += 1
                    o.sigval = c
        for s, lst in self.stream_ops.items():
            for k, o in enumerate(lst):
                o.sigval = 16 * (k + 1)

    def emit(self, nc, block, sems):
        if LIST_SCHED:
            self.list_schedule(window=LS_WINDOW, hop=LS_HOP)
        self.finalize()
        sched = self

        def replay(ename, eng):
            waited = {}
            for o in sched.q[ename]:
                need = {}
                for d in o.deps:
                    if d.stream is None:
                        if d.eng == ename and (ename in SELF_ORDERED):
                            continue
                        key = d.eng
                    else:
                        key = d.stream
                    if d.sigval > need.get(key, 0):
                        need[key] = d.sigval
                for key, val in need.items():
                    if waited.get(key, 0) >= val:
                        continue
                    waited[key] = val
                    eng.wait_ge(sems[key], val)
                ins = o.fn(eng)
                if o.signal:
                    if o.stream is None:
                        ins.then_inc(sems[ename], 1)
                    else:
                        ins.then_inc(sems[o.stream], 16)

        @block.tensor
        def _(eng):
            replay("pe", eng)

        @block.scalar
        def _(eng):
            replay("act", eng)

        @block.vector
        def _(eng):
            replay("dve", eng)

        @block.gpsimd
        def _(eng):
            replay("pool", eng)

        @block.sync
        def _(eng):
            replay("sp", eng)


def _rel_bucket(d):
    n = np.maximum(d, 0).astype(np.int32)
    max_exact = 16
    nf = np.maximum(n, 1).astype(np.float32)
    large = max_exact + (np.log(nf / np.float32(max_exact)) / np.float32(math.log(128 / max_exact))
                         * np.float32(32 - max_exact)).astype(np.int32)
    large = np.minimum(large, 31)
    return np.where(n < max_exact, n, large)


def _static_consts():
    c = {}
    c["ident"] = np.eye(128, dtype=np.float32)
    d = np.arange(128)
    bk = _rel_bucket(d)
    oh = np.zeros((32, 128), np.float32)
    oh[bk, d] = 1.0
    c["onehot"] = oh
    m = np.arange(256)
    R = np.zeros((128, 256), np.float32)
    R[(128 - m) % 128, m] = 1.0
    c["rsel"] = R
    j = np.arange(128)[:, None]
    i = np.arange(128)[None, :]
    c["mcur"] = np.where(j <= i, 0.0, MASK).astype(np.float32)
    c["mprev"] = np.where(j > i, 0.0, MASK).astype(np.float32)
    w = np.array([2, 4, 8, 16])[:, None]
    pos = np.arange(16)[None, :]
    cnt = np.minimum(w, pos + 1).astype(np.float32)
    c["cnt"] = np.broadcast_to(cnt.reshape(1, 64), (128, 64)).copy()
    return c


def build_nc():
    nc = bass.Bass("TRN2", target_bir_lowering=False)

    def din(name, shape):
        return nc.dram_tensor(name, list(shape), F32, kind="ExternalInput").ap()

    def dout(name, shape):
        return nc.dram_tensor(name, list(shape), F32, kind="ExternalOutput").ap()

    xp = din("xp", [SEQ, D])
    xs = din("xs", [NS, D])
    kc_d = din("kc", [NB, 128, 128])
    vc_d = din("vc", [NB, 128, 128])
    spl_d = din("spl", [NB * 15, 512])
    meta_d = din("meta", [16, D])
    lnp_d = din("lnp", [6, D])
    rel_d = din("rel", [32, 8])
    win_d = din("win", [D, 1280])
    wpool_d = din("wpool", [4, 128, 128])
    pscale_d = din("pscale", [128, 4])
    esink_d = din("sinkp", [128, 4])
    wout_d = din("wout", [D, D])
    w1_d = din("w1", [D, 4096])
    w2_d = din("w2", [4096, D])
    ident_d = din("ident", [128, 128])
    onehot_d = din("onehot", [32, 128])
    rsel_d = din("rsel", [128, 256])
    mcur_d = din("mcur", [128, 128])
    mprev_d = din("mprev", [128, 128])
    cnt_d = din("cnt", [128, 64])

    y_p = dout("y_p", [SEQ, D])
    y_s = dout("y_s", [NS, D])
    nk_p = dout("nk_p", [128, 128])
    nv_p = dout("nv_p", [128, 128])
    np_p = dout("np_p", [15, 512])
    nk_s = dout("nk_s", [NB, 128, 128])
    nv_s = dout("nv_s", [NB, 128, 128])
    np_s = dout("np_s", [NB, 15, 512])

    import os as _os0
    KSKIP = set(_os0.environ.get("KSKIP", "").split(","))
    S = Sched()
    from contextlib import ExitStack
    es = ExitStack()

    ARENA_F32 = 53200
    arena = es.enter_context(nc.sbuf_tensor("arena", [128, ARENA_F32], F32))
    psum = es.enter_context(nc.psum_tensor("ps", [128, 8 * 512], F32))

    class Bump:
        def __init__(self, start=0):
            self.off = start

        def f32(self, n):
            a = arena[:, self.off:self.off + n]
            self.off += n
            assert self.off <= ARENA_F32, self.off
            return a

        def bf16(self, n):
            assert n % 2 == 0
            a = arena[:, self.off:self.off + n // 2].bitcast(BF16)
            self.off += n // 2
            assert self.off <= ARENA_F32, self.off
            return a

    def bank(b, n=512, off=0):
        return psum[:, b * 512 + off: b * 512 + off + n]

    BK = [Res("bank%d" % b) for b in range(8)]
    psb = psum.bitcast(BF16)

    P = Bump(0)
    S_main = P.f32(17 * 1024)
    h1T = P.bf16(8 * 2112)
    h1T3 = h1T.rearrange("p (k t) -> p k t", k=8)
    R_S = [Res("S%d" % i) for i in range(17)]
    R_h1T = [Res("h1T%d" % i) for i in range(17)]
    mhalf = P.f32(2)[:, 0:1]
    persist_end = P.off

    def Stile(i):
        return S_main[:, i * 1024:(i + 1) * 1024]

    A = Bump(persist_end)
    w_in = A.bf16(8 * 1280).rearrange("p (k f) -> p k f", k=8)
    w_out = A.bf16(8 * 1024).rearrange("p (k f) -> p k f", k=8)
    w_pool = A.bf16(4 * 128).rearrange("p (g f) -> p g f", g=4)
    G0 = A.f32(1024); B0 = A.f32(1024); G1 = A.f32(1024); B1 = A.f32(1024)
    ident_f = A.f32(128)
    ident_b = A.bf16(128)
    ones_b = A.bf16(64)
    Bcur = A.bf16(1024)
    Bprev = A.bf16(1024)
    pscale = A.f32(4)
    esink = A.f32(4)
    rec_b = A.f32(512)
    mcur = rec_b[:, 0:128]; mprev = rec_b[:, 128:256]
    cnt = A.f32(64)
    rcnt = A.f32(64)
    tbT = A.f32(8)
    rel_sb = A.f32(8)
    onehot = A.f32(128)
    rsel_b = A.bf16(256)
    tbT_b = A.bf16(8)
    S0 = A.f32(1024)
    h0T = A.bf16(8 * 128).rearrange("p (k t) -> p k t", k=8)
    kT = [A.bf16(128), A.bf16(128)]
    vtok = [A.bf16(128), A.bf16(128)]
    mixT = A.bf16(8 * 128).rearrange("p (k t) -> p k t", k=8)
    pTe = [[A.bf16(512), A.bf16(512)], [A.bf16(512), A.bf16(512)]]
    rec = A.f32(512)
    stg = A.f32(768)
    stat = A.f32(16)
    uT = A.f32(4 * 144).rearrange("p (g t) -> p g t", g=4)
    off_wsA = A.off
    wsA = A.f32(4 * 144).rearrange("p (g t) -> p g t", g=4)
    off_wsB = A.off
    wsB = A.f32(4 * 144).rearrange("p (g t) -> p g t", g=4)
    ppT = A.bf16(4 * 128).rearrange("p (g t) -> p g t", g=4)
    dtmp = A.f32(64)
    off_r1 = A.off
    h0T_b = A.bf16(8 * 128).rearrange("p (k t) -> p k t", k=8)
    qT_b = A.bf16(4 * 128)
    qT = qT_b
    hbuf = A.bf16(1024)
    qzp = [[A.bf16(512), A.bf16(512)], [A.bf16(512), A.bf16(512)]]
    assert A.off - off_r1 >= 2048
    off_r2 = A.off
    mixT_b = A.bf16(8 * 128).rearrange("p (k t) -> p k t", k=8)
    pTe_b = [[A.bf16(512), A.bf16(512)], [A.bf16(512), A.bf16(512)]]
    stat_b = A.f32(16)
    ppT_b = A.bf16(4 * 128).rearrange("p (g t) -> p g t", g=4)
    uT_b = A.f32(4 * 144).rearrange("p (g t) -> p g t", g=4)
    assert A.off - off_r2 >= 2048
    kT.append(A.bf16(128)); vtok.append(A.bf16(128))
    CTX = [dict(h0T=h0T, qT=qT, mixT=mixT, pTe=pTe, rec=rec, stat=stat, ppT=ppT, uT=uT, sfx=""),
           dict(h0T=h0T_b, qT=qT_b, mixT=mixT_b, pTe=pTe_b, rec=rec_b, stat=stat_b, ppT=ppT_b, uT=uT_b, sfx="_b")]
    CUR = dict(CTX[0])
    PX = [""]

    def use(p):
        CUR.clear(); CUR.update(CTX[p]); PX[0] = CTX[p]["sfx"]
    phase1_end = A.off

    Q = Bump(7 * 1024)
    kcb = Q.bf16(16 * 128).rearrange("p (b f) -> p b f", b=16)
    kcT = Q.bf16(16 * 128).rearrange("p (b f) -> p b f", b=16)
    vcb = Q.bf16(16 * 128).rearrange("p (b f) -> p b f", b=16)
    uext = Q.f32(4 * 16 * 19).rearrange("p (g b t) -> p g b t", g=4, b=16)
    xsA = Q.f32(4 * 16 * 19).rearrange("p (g b t) -> p g b t", g=4, b=16)
    xsB = Q.f32(4 * 16 * 19).rearrange("p (g b t) -> p g b t", g=4, b=16)
    spl = Q.f32(2 * 512).rearrange("p (h f) -> p h f", h=2)
    Bs = Q.bf16(512)
    Bn = Q.bf16(512)
    pTc = Q.bf16(512)
    pTn = Q.bf16(512)
    ppTs = Q.bf16(4 * 64).rearrange("p (g t) -> p g t", g=4)
    qz = [Q.bf16(4 * 64), Q.bf16(4 * 64)]
    assert Q.off <= 16 * 1024, Q.off

    R = {}

    def res(n):
        if n not in R:
            R[n] = Res(n)
        return R[n]

    def fsz(ap):
        n = 1
        for s_ in ap.shape[1:]:
            n *= s_
        return n

    def dma(eng, out, in_, stream, reads=(), writes=(), free_writes=()):
        return S.op(eng, lambda e: e.dma_start(out=out, in_=in_), reads=reads, writes=writes,
                    stream=stream, free_writes=free_writes, cost=fsz(out) * out.shape[0] * 4 / 3.0e5)

    def act(out, in_, func, reads, writes, **kw):
        return S.op("act", lambda e: e.activation(out=out, in_=in_, func=func, **kw), reads=reads, writes=writes,
                    cost=0.2 + fsz(out) / 1.0e3)

    def mm(out, lhsT, rhs, start, stop, reads, writes):
        return S.op("pe", lambda e: e.matmul(out, lhsT, rhs, start=start, stop=stop), reads=reads, writes=writes,
                    cost=0.012 + max(fsz(rhs), 64) / 2.4e3, gstart=bool(start), gstop=bool(stop))

    def tr(out, in_, ident, reads, writes):
        return S.op("pe", lambda e: e.transpose(out, in_, ident), reads=reads, writes=writes, cost=0.11)

    def tt(eng, out, in0, in1, op, reads, writes):
        return S.op(eng, lambda e: e.tensor_tensor(out=out, in0=in0, in1=in1, op=op), reads=reads, writes=writes,
                    cost=(0.08 + fsz(out) / 0.96e3) if eng == "dve" else (0.3 + fsz(out) / 0.5e3))

    def ts(eng, out, in0, s1, s2, op0, op1, reads, writes):
        c_ = 0.08 + fsz(out) / 0.96e3
        if op1 is None:
            return S.op(eng, lambda e: e.tensor_scalar(out, in0, s1, None, op0), reads=reads, writes=writes, cost=c_)
        return S.op(eng, lambda e: e.tensor_scalar(out, in0, s1, s2, op0, op1), reads=reads, writes=writes, cost=c_)

    def stt(eng, out, in0, scalar, in1, op0, op1, reads, writes):
        return S.op(eng, lambda e: e.scalar_tensor_tensor(out=out, in0=in0, scalar=scalar, in1=in1, op0=op0, op1=op1),
                    reads=reads, writes=writes, cost=(0.1 + fsz(out) / 0.8e3) if eng == "dve" else (0.3 + fsz(out) / 0.5e3))

    def cp(eng, out, in_, reads, writes):
        return S.op(eng, lambda e: e.tensor_copy(out=out, in_=in_), reads=reads, writes=writes, cost=0.08 + fsz(out) / 0.96e3)

    def mset(eng, ap, val, writes):
        return S.op(eng, lambda e: e.memset(ap, val), writes=writes, cost=0.06 + fsz(ap) / 1.9e3)

    r_par = res("params")
    r_p0 = res("p0"); r_p1 = res("p1")
    r_xs = R_S[16]
    r_S0 = res("S0")
    r_kcb = res("kcb"); r_vcb = res("vcb"); r_spl = res("spl")
    r_win = res("w_in"); r_wout = res("w_out"); r_wpool = res("w_pool")
    dma("sp", Stile(16)[0:NS, :], xs, "x16", writes=[r_xs])
    dma("sp", G0, lnp_d[0:1, :].partition_broadcast(128), "p0", free_writes=[r_p0])
    dma("sp", B0, lnp_d[1:2, :].partition_broadcast(128), "p0", free_writes=[r_p0])
    par_list = [
        (ident_f, ident_d), (onehot[0:32, :], onehot_d), (mcur, mcur_d), (mprev, mprev_d),
        (cnt, cnt_d), (rel_sb[0:32, :], rel_d), (pscale, pscale_d), (esink, esink_d),
    ]
    for o_, i_ in par_list:
        dma("sp", o_, i_, "params", free_writes=[r_par])
    win_v = win_d.rearrange("(k p) f -> p k f", p=128)
    wout_v = wout_d.rearrange("(k p) f -> p k f", p=128)
    for kk in range(0, 8, 2):
        dma("pool", w_in[:, kk:kk + 2, :], win_v[:, kk:kk + 2, :], "w_in", free_writes=[r_win])
    dma("pool", rsel_b, rsel_d, "rselb", writes=[res("rsel_b")])
    dma("pool", w_pool, wpool_d.rearrange("g c d -> c g d"), "w_pool", writes=[r_wpool])
    for b4 in range(0, NB, 2):
        dma("pool", kcb[:, b4:b4 + 2, :], kc_d[b4:b4 + 2].rearrange("b k f -> k b f"), "kc", free_writes=[r_kcb])
        dma("pool", vcb[:, b4:b4 + 2, :], vc_d[b4:b4 + 2].rearrange("b k f -> k b f"), "vc", free_writes=[r_vcb])
    for kk in range(0, 8, 2):
        dma("pool", w_out[:, kk:kk + 2, :], wout_v[:, kk:kk + 2, :], "w_out", free_writes=[r_wout])
    dma("sp", spl[0:120, :, :], spl_d.rearrange("(h r) f -> r h f", h=2), "spl", writes=[r_spl])
    mset("dve", S0[0:96, :], 0.0, writes=[r_S0])
    mset("dve", S0[96:128, :], 0.0, writes=[r_S0])
    dma("sp", S0[112:128, :], meta_d, "x0", writes=[r_S0])
    dma("sp", Stile(0), xp[0:128, :], "xt0", writes=[R_S[0]])
    dma("sp", G1, lnp_d[2:3, :].partition_broadcast(128), "p1", free_writes=[r_p1])
    dma("sp", B1, lnp_d[3:4, :].partition_broadcast(128), "p1", free_writes=[r_p1])

    r_c = res("consts")
    cp("dve", ident_b, ident_f, [r_par], [r_c])
    mset("dve", ones_b, 1.0, [res("ones")])
    for p_ in range(2):
        mset("dve", qzp[p_][0][64:128, :], 0.0, [res("qz%d" % p_)])
        mset("dve", qzp[p_][1][0:64, :], 0.0, [res("qz%d" % p_)])
    mset("dve", mhalf, -0.5, [res("mhalf")])
    S.op("dve", lambda e: e.reciprocal(rcnt, cnt), reads=[r_par], writes=[res("rcnt")])
    act(esink, esink, AF.Exp, [r_par], [r_par])
    mm(bank(0, 8), onehot[0:32, :], rel_sb[0:32, :], True, True, [r_par], [BK[0]])
    cp("dve", tbT, bank(0, 8), [BK[0]], [res("tbT")])
    cp("dve", tbT_b, tbT, [res("tbT")], [res("tbT_b")])
    for i in range(128):
        b_ = 2 + (i // 64)
        mm(bank(b_, 8, (i % 64) * 8), rsel_b[:, 128 - i:256 - i], tbT_b, True, True, [res("rsel_b"), res("tbT_b")], [BK[b_]])
    r_B = res("Btiles")
    for h in range(8):
        for hb in range(2):
            src = psum[:, (2 + hb) * 512:(3 + hb) * 512].rearrange("p (i h) -> p h i", h=8)[:, h, :]
            tt("dve", Bcur[:, h * 128 + hb * 64: h * 128 + hb * 64 + 64], src, mcur[:, hb * 64:(hb + 1) * 64], ALU.add,
               [BK[2 + hb], r_par], [r_B])
            tt("dve", Bprev[:, h * 128 + hb * 64: h * 128 + hb * 64 + 64], src, mprev[:, hb * 64:(hb + 1) * 64], ALU.add,
               [BK[2 + hb], r_par], [r_B])
    r_Bs = res("Bs")
    Bs5 = Bs.rearrange("p (k b g t) -> p k b g t", k=2, b=16, g=4)
    Bprev4 = Bprev.rearrange("p (k g i) -> p k g i", k=2, g=4)
    Bcur4 = Bcur.rearrange("p (k g i) -> p k g i", k=2, g=4)
    for b in range(NB):
        cp("dve", Bs5[:, :, b, :, :], Bprev4[:, :, :, 0:4], [r_B], [r_Bs])
    r_Bn0 = res("Bn_init"); r_Bn = res("Bn")
    mset("dve", Bn[0:64, :], MASK, [r_Bn0])
    mset("dve", Bn[64:128, :], 0.0, [r_Bn0])
    Bn5 = Bn.rearrange("p (k b g t) -> p k b g t", k=2, b=16, g=4)
    for b in range(NB if "bn" not in KSKIP else 0):
        for k_ in range(2):
            dma("sp", Bn5[4 * b:4 * b + 4, k_, b, :, :], Bcur4[0:4, k_, :, 0:4], "bn", reads=[r_B, r_Bn0], free_writes=[r_Bn])
    KXW = int(_os0.environ.get("KXW", "0"))
    for i in range(1, 7):
        dma("sp", Stile(i), xp[i * 128:(i + 1) * 128, :], "xt%d" % i, reads=[r_win] if (i == KXW) else (), writes=[R_S[i]])
    dma("sp", nk_s[:, 0:124, :], kc_d[:, 4:128, :], "d2d")
    dma("sp", nv_s[:, 0:124, :], vc_d[:, 4:128, :], "d2d")
    dma("sp", np_s[:, 0:11, :], spl_d.rearrange("(b j) f -> b j f", j=15)[:, 4:15, :], "d2d")


    import os as _os
    KSTOP = _os.environ.get("KSTOP", "")
    KLN = _os0.environ.get("KLN", "stt")
    KLN1 = _os0.environ.get("KLN1", "stt")

    def layernorm(Sap, rows, G, Bt, rS, engine_gb="pool", stat_ap=None, r_g=None, mode=None):
        r_st = res("stat" + PX[0]) if stat_ap is None else res("stat2_%d" % id(stat_ap))
        stat = stat_ap if stat_ap is not None else CUR["stat"]
        x = Sap[0:rows, :]
        for c2 in range(2):
            S.op("dve", lambda e, c2=c2: e.bn_stats(stat[0:rows, c2 * 6:(c2 + 1) * 6], Sap[0:rows, c2 * 512:(c2 + 1) * 512]),
                 reads=[rS], writes=[r_st])
        S.op("dve", lambda e: e.bn_aggr(stat[0:rows, 12:14], stat[0:rows, 0:12]), reads=[r_st], writes=[r_st])
        ts("dve", stat[0:rows, 15:16], stat[0:rows, 13:14], EPS, None, ALU.add, None, [r_st], [r_st])
        tt("pool", stat[0:rows, 14:15], stat[0:rows, 15:16], mhalf[0:rows, :], ALU.pow, [r_st, res("mhalf")], [r_st])
        if (mode or KLN) == "mix":
            stt("dve", x, x, stat[0:rows, 12:13], G[0:rows, :], ALU.subtract, ALU.mult, [rS, r_st, r_g or r_par], [rS])
            stt("pool", x, x, stat[0:rows, 14:15], Bt[0:rows, :], ALU.mult, ALU.add, [rS, r_st, r_g or r_par], [rS])
        elif (mode or KLN) == "stt":
            stt("dve", x, x, stat[0:rows, 12:13], G[0:rows, :], ALU.subtract, ALU.mult, [rS, r_st, r_g or r_par], [rS])
            stt("dve", x, x, stat[0:rows, 14:15], Bt[0:rows, :], ALU.mult, ALU.add, [rS, r_st, r_g or r_par], [rS])
        else:
            ts("dve", x, x, stat[0:rows, 12:13], stat[0:rows, 14:15], ALU.subtract, ALU.mult, [rS, r_st], [rS])
            tt(engine_gb, x, x, G[0:rows, :], ALU.mult, [rS, r_g or r_par], [rS])
            tt(engine_gb, x, x, Bt[0:rows, :], ALU.add, [rS, r_g or r_par], [rS])

    KTR = _os0.environ.get("KTR", "f32")

    def transpose_to(Sap, rows, rS, dst3, tok0, rdst):
        if KTR == "f32":
            for kc in range(8):
                b_ = kc // 4
                tr(bank(b_, rows, (kc % 4) * 128), Sap[0:rows, kc * 128:(kc + 1) * 128], ident_f[0:rows, 0:rows],
                   [rS, r_par], [BK[b_]])
            for b_ in range(2):
                src_ = bank(b_).rearrange("p (k t) -> p k t", k=4)[:, :, 0:rows]
                act(dst3[:, b_ * 4:(b_ + 1) * 4, tok0:tok0 + rows], src_, AF.Copy, [BK[b_]], [rdst])
            return
        r_hb = res("hb")
        if KTR == "bf16dve":
            cp("dve", hbuf[0:rows, :], Sap[0:rows, :], [rS], [r_hb])
        else:
            act(hbuf[0:rows, :], Sap[0:rows, :], AF.Copy, [rS], [r_hb])
        for kc in range(8):
            tr(psb[:, kc * 128:kc * 128 + rows], hbuf[0:rows, kc * 128:(kc + 1) * 128], ident_b[0:rows, 0:rows],
               [r_hb, r_c], [BK[0]])
        src_ = psb[:, 0:1024].rearrange("p (k t) -> p k t", k=8)[:, :, 0:rows]
        act(dst3[:, 0:8, tok0:tok0 + rows], src_, AF.Copy, [BK[0]], [rdst])

    KIPO = _os0.environ.get("KIPO", "ufirst")

    def in_proj(rows, r_h0T, r_uT, uT_dst, r_qT, kT_dst, r_kT, v_dst, r_v, tokmajor_extra, q_dst=None, skip_q=False):
        h0T = CUR["h0T"]; qT = CUR["qT"]
        for oc in ((4, 5, 6, 7, 8, 0, 1, 2, 3) if KIPO == "qfirst" else range(9)):
            if skip_q and 4 <= oc < 8:
                continue
            b_ = 2 if oc < 4 else (3 if oc < 8 else 4)
            off = (oc % 4) * 128 if oc < 8 else 0
            for kc in range(8):
                mm(bank(b_, rows, off), w_in[:, kc, oc * 128:(oc + 1) * 128], h0T[:, kc, 0:rows], kc == 0, kc == 7,
                   [r_win, r_h0T], [BK[b_]])
        for kc in range(8):
            mm(bank(4, 128, 128)[0:rows, :], h0T[:, kc, 0:rows], w_in[:, kc, 1152:1280], kc == 0, kc == 7,
               [r_win, r_h0T], [BK[4]])
        if tokmajor_extra:
            for kc in range(8):
                mm(bank(4, 128, 256)[0:rows, :], h0T[:, kc, 0:rows], w_in[:, kc, 1024:1152], kc == 0, kc == 7,
                   [r_win, r_h0T], [BK[4]])
            for kc in range(8):
                mm(bank(5)[0:rows, :], h0T[:, kc, 0:rows], w_in[:, kc, 0:512], kc == 0, kc == 7,
                   [r_win, r_h0T], [BK[5]])
        if KIPO != "qfirst":
            uT_dst(bank(2).rearrange("p (g t) -> p g t", g=4)[:, :, 0:rows])
        if skip_q:
            pass
        elif q_dst is not None:
            q_dst()
        else:
            act(qT.rearrange("p (g t) -> p g t", g=4)[:, :, 0:rows], bank(3).rearrange("p (g t) -> p g t", g=4)[:, :, 0:rows],
                AF.Copy, [BK[3]], [r_qT], scale=0.125)
        if KIPO == "qfirst":
            uT_dst(bank(2).rearrange("p (g t) -> p g t", g=4)[:, :, 0:rows])
        cp("dve", kT_dst[:, 0:rows], bank(4, rows, 0), [BK[4]], [r_kT])
        cp("dve", v_dst[0:rows, :], bank(4, 128, 128)[0:rows, :], [BK[4]], [r_v])
        if tokmajor_extra:
            r_stg = res("stg")
            cp("dve", stg[0:rows, 512:640], bank(4, 128, 256)[0:rows, :], [BK[4]], [r_stg])
            cp("dve", stg[0:rows, 640:768], bank(4, 128, 128)[0:rows, :], [BK[4]], [r_stg])
            act(stg[0:rows, 0:512], bank(5)[0:rows, :], AF.Copy, [BK[5]], [r_stg])

    def window_sums(u4, wa, wb, L, r_u, r_ws, eng="dve"):
        def sl(ap, g0, g1, a, b):
            return ap[:, g0:g1, ..., a:b] if False else ap[(slice(None), slice(g0, g1)) + (slice(None),) * (len(ap.shape) - 3) + (slice(a, b),)]
        tt(eng, sl(wa, 0, 4, 1, L), sl(u4, 0, 4, 1, L), sl(u4, 0, 4, 0, L - 1), ALU.add, [r_u], [r_ws])
        tt(eng, sl(wb, 1, 4, 3, L), sl(wa, 1, 4, 3, L), sl(wa, 1, 4, 1, L - 2), ALU.add, [r_ws], [r_ws])
        tt(eng, sl(wa, 2, 4, 7, L), sl(wb, 2, 4, 7, L), sl(wb, 2, 4, 3, L - 4), ALU.add, [r_ws], [r_ws])
        tt(eng, sl(wb, 3, 4, 15, L), sl(wa, 3, 4, 15, L), sl(wa, 3, 4, 7, L - 8), ALU.add, [r_ws], [r_ws])

    WIN = (2, 4, 8, 16)

    def pool_mm(ppT_ap, rows, r_pp, r_mix):
        mixT = CUR["mixT"]
        for g in range(4):
            mm(bank(5, rows, g * 128), w_pool[:, g, :], ppT_ap[:, g, 0:rows], True, True, [r_wpool, r_pp], [BK[5]])
        for g in range(4):
            act(mixT[:, g, 0:rows], bank(5, rows, g * 128), AF.Copy, [BK[5], r_par], [r_mix], scale=pscale[:, g:g + 1])

    def attn_finish(ncols, view, mix_out, r_mix):
        r_rec = res("rec" + PX[0])
        rec = CUR["rec"]
        if "actrec" in KSKIP:
            for g in range(4):
                ts("dve", view(rec[:, 0:ncols])[:, g], view(bank(5, ncols))[:, g], esink[:, g:g + 1], None, ALU.add, None,
                   [BK[5], r_par], [r_rec])
            S.op("dve", lambda e: e.reciprocal(rec[:, 0:ncols], rec[:, 0:ncols]), reads=[r_rec], writes=[r_rec])
        else:
            for g in range(4):
                act(view(rec[:, 0:ncols])[:, g], view(bank(5, ncols))[:, g], AF.Ln, [BK[5], r_par], [r_rec], bias=esink[:, g:g + 1])
            act(rec[:, 0:ncols], rec[:, 0:ncols], AF.Exp, [r_rec], [r_rec], scale=-1.0)
        tt("dve", mix_out, view(bank(4, ncols)), view(rec[:, 0:ncols]), ALU.mult, [BK[4], r_rec], [r_mix])

    def out_proj_ln1(Sap, rows, rS, r_mix, dst_tok0, r_dst, defer_T=False):
        mixT = CUR["mixT"]
        for half in range(2):
            for e_ in range(8):
                mm(bank(half)[0:rows, :], mixT[:, e_, 0:rows], w_out[:, e_, half * 512:(half + 1) * 512], e_ == 0, e_ == 7,
                   [r_wout, r_mix], [BK[half]])
        for half in range(2):
            stt("dve", Sap[0:rows, half * 512:(half + 1) * 512], Sap[0:rows, half * 512:(half + 1) * 512], ALPHA,
                bank(half)[0:rows, :], ALU.mult, ALU.add, [rS, BK[half]], [rS])
        layernorm(Sap, rows, G1, B1, rS, r_g=r_p1, mode=KLN1)
        if dst_tok0 is not None and not defer_T:
            transpose_to(Sap, rows, rS, h1T3, dst_tok0, r_dst)

    def finish():
        S.fence()
        S.op("sp", lambda e: e.nop())
        names = list(Sched.ENGS) + list(S.stream_ops.keys())
        sems = {}
        for nm in names:
            sems[nm] = es.enter_context(nc.semaphore("s_" + nm))
        block = es.enter_context(nc.Block())
        S.emit(nc, block, sems)
        es.close()
        return nc
    if KSTOP == "A":
        return finish()
    Ss = Stile(16)
    rSs = R_S[16]
    SSL = 2
    r_kT = [res("kT0"), res("kT1"), res("kT2")]; r_v = [res("v0"), res("v1"), res("v2")]
    RP = [dict(h0T=res("h0T"), qT=res("qT"), mix=res("mixT"), pTe=res("pTe"), pp=res("ppT"), uT=res("uT")),
          dict(h0T=res("h0T_b"), qT=res("qT_b"), mix=res("mixT_b"), pTe=res("pTe_b"), pp=res("ppT_b"), uT=res("uT_b"))]
    mset("dve", vtok[SSL][64:128, :], 0.0, [r_v[SSL]])
    mset("dve", pTn[64:128, :], 0.0, [res("pTs")])

    def sample_gen():
        C = CTX[1]
        rp = RP[1]
        r_uext = res("uext")
        qTs = C["qT"]
        qT4 = qTs.rearrange("p (g t) -> p g t", g=4)
        mixTs = C["mixT"]
        use(1)
        layernorm(Ss, NS, G0, B0, rSs, r_g=r_p0)
        yield
        use(1)
        transpose_to(Ss, NS, rSs, C["h0T"], 0, rp["h0T"])
        for hh in range(2):
            for g in range(4):
                tr(bank(6 + hh, 120, g * 128), spl[0:120, hh, g * 128:(g + 1) * 128], ident_f[0:120, 0:120], [r_spl, r_par], [BK[6 + hh]])
            for g in range(4):
                cp("dve", uext[:, g, hh * 8:(hh + 1) * 8, 0:15], bank(6 + hh, 120, g * 128).rearrange("p (b j) -> p b j", j=15),
                   [BK[6 + hh]], [r_uext])
        r_kcT = res("kcT")
        for b in range(NB):
            b_ = 6 + b // 8
            o_ = psb[:, b_ * 1024 + (b % 8) * 128: b_ * 1024 + (b % 8) * 128 + 128]
            tr(o_, kcb[:, b, :], ident_b, [r_kcb, r_c], [BK[b_]])
        for hh in range(2):
            cp("dve", kcT[:, hh * 8:(hh + 1) * 8, :], psb[:, (6 + hh) * 1024:(7 + hh) * 1024].rearrange("p (b k) -> p b k", b=8),
               [BK[6 + hh]], [r_kcT])
        yield
        use(1)

        def uT_dst_sample(src_):
            for g in range(4):
                act(uext[:, g, :, 15:19], src_[:, g, :].rearrange("p (b t) -> p b t", t=4), AF.Copy, [BK[2]], [r_uext])
        in_proj(NS, rp["h0T"], r_uext, uT_dst_sample, rp["qT"], kT[SSL], r_kT[SSL], vtok[SSL], r_v[SSL], True)
        r_stg = res("stg")
        dma("sp", np_s[:, 11:15, :], stg[0:NS, 0:512], "outs", reads=[r_stg])
        dma("sp", nk_s[:, 124:128, :], stg[0:NS, 512:640], "outs", reads=[r_stg])
        dma("sp", nv_s[:, 124:128, :], stg[0:NS, 640:768], "outs", reads=[r_stg])
        yield
        use(1)
        r_xs_ws = res("xs_ws")
        window_sums(uext, xsA, xsB, 19, r_uext, r_xs_ws)
        r_pps = res("ppTs")
        fin = [xsA, xsB, xsA, xsB]
        for g in range(4):
            stt("dve", ppTs[:, g, :].rearrange("p (b t) -> p b t", t=4), fin[g][:, g, :, 15:19], 1.0 / WIN[g],
                uext[:, g, :, 15:19], ALU.mult, ALU.subtract, [r_xs_ws, r_uext], [r_pps])
        pool_mm(ppTs, NS, r_pps, rp["mix"])
        r_qz = res("qz")
        for kvh in range(2):
            rows_ = slice(kvh * 64, kvh * 64 + 64)
            mset("dve", qz[kvh], 0.0, [r_qz])
            cp("dve", qz[kvh][rows_, :].rearrange("p (g t) -> p g t", g=4), qT4[rows_, :, 0:NS], [rp["qT"]], [r_qz])
        for kvh in range(2):
            rows_ = slice(kvh * 64, kvh * 64 + 64)
            bc = 6 + kvh
            bn_ = 2 + kvh
            mm(bank(bc, 256), ident_b, Bs[:, kvh * 256:(kvh + 1) * 256], True, False, [r_c, r_Bs], [BK[bc]])
            for b in range(NB):
                mm(bank(bc, 16, b * 16), kcT[rows_, b, :], qT4[rows_, :, 4 * b:4 * b + 4], False, b == NB - 1,
                   [r_kcT, rp["qT"]], [BK[bc]])
            mm(bank(bn_, 256)[0:NS, :], ident_b[:, 0:NS], Bn[:, kvh * 256:(kvh + 1) * 256], True, False, [r_c, r_Bn, r_Bn0], [BK[bn_]])
            qzb = qz[kvh].rearrange("p (g b t) -> p b g t", g=4, t=4)
            mm(bank(bn_, 256)[0:NS, :], kT[SSL][:, 0:NS], qzb, False, True, [r_kT[SSL], r_qz], [BK[bn_]])
        r_pTs = res("pTs")
        for kvh in range(2):
            act(pTc[:, kvh * 256:(kvh + 1) * 256], bank(6 + kvh, 256), AF.Exp, [BK[6 + kvh]], [r_pTs])
            act(pTn[0:NS, kvh * 256:(kvh + 1) * 256], bank(2 + kvh, 256)[0:NS, :], AF.Exp, [BK[2 + kvh]], [r_pTs])
        yield
        use(1)
        for kvh in range(2):
            rows_ = slice(kvh * 64, kvh * 64 + 64)
            mm(bank(4, 256)[rows_, :], vtok[SSL][:, kvh * 64:(kvh + 1) * 64], pTn[:, kvh * 256:(kvh + 1) * 256], True, False,
               [r_v[SSL], r_pTs], [BK[4]])
            for b in range(NB):
                mm(bank(4, 16, b * 16)[rows_, :], vcb[:, b, kvh * 64:(kvh + 1) * 64], pTc[:, kvh * 256 + b * 16:kvh * 256 + b * 16 + 16],
                   False, b == NB - 1, [r_vcb, r_pTs], [BK[4]])
            mm(bank(5, 256)[rows_, :], ones_b[:, 0:64], pTc[:, kvh * 256:(kvh + 1) * 256], True, False, [res("ones"), r_pTs], [BK[5]])
            mm(bank(5, 256)[rows_, :], ones_b[:, 0:64], pTn[:, kvh * 256:(kvh + 1) * 256], False, True,
               [res("ones"), r_pTs], [BK[5]])
        attn_finish(256, lambda a: a.rearrange("p (b g t) -> p g b t", b=16, g=4),
                    mixTs[:, 4:8, 0:NS].rearrange("p g (b t) -> p g b t", t=4), rp["mix"])
        yield
        use(1)
        out_proj_ln1(Ss, NS, rSs, rp["mix"], 2048, R_h1T[16], defer_T=True)
        yield
        use(1)
        transpose_to(Ss, NS, rSs, h1T3, 2048, R_h1T[16])

    def after_sample():
        S.fence(only=("sp",))
        for i in range(7, 16):
            dma("sp", Stile(i), xp[i * 128:(i + 1) * 128, :], "xt%d" % i, writes=[R_S[i]])

    r_ws = res("ws")
    mset("dve", CTX[0]["uT"][:, :, 0:16], 0.0, [RP[0]["uT"]])

    KPOOLENG = _os.environ.get("KPOOLENG", "dve")

    def tile_gen(t):
        p = t % 2
        rp = RP[p]
        Sap = S0 if t == 0 else Stile(t - 1)
        rS = r_S0 if t == 0 else R_S[t - 1]
        cur = t % 3
        prv = (t - 1) % 3
        last = (t == NTILE)
        C = CTX[p]
        uT_ = C["uT"]; ppT_ = C["ppT"]; pTe_ = C["pTe"]; mixT_ = C["mixT"]
        qT4_ = C["qT"].rearrange("p (g t) -> p g t", g=4)
        use(p)
        layernorm(Sap, 128, G0, B0, rS, r_g=r_p0)
        yield
        use(p)
        transpose_to(Sap, 128, rS, C["h0T"], 0, rp["h0T"])
        yield
        use(p)

        def uT_dst_p(src_):
            act(uT_[:, :, 16:144], src_, AF.Copy, [BK[2]], [rp["uT"]])
        r_qz_ = res("qz%d" % p)

        def q_dst_p():
            act(qzp[p][0][0:64, :], bank(3)[0:64, :], AF.Copy, [BK[3]], [r_qz_], scale=0.125)
            act(qzp[p][1][64:128, :], bank(3)[64:128, :], AF.Copy, [BK[3]], [r_qz_], scale=0.125)
        in_proj(128, rp["h0T"], rp["uT"], uT_dst_p, rp["qT"], kT[cur], r_kT[cur], vtok[cur], r_v[cur], last, q_dst=q_dst_p, skip_q=(t == 0))
        if last:
            r_stg = res("stg")
            dma("sp", np_p, stg[113:128, 0:512], "outp", reads=[r_stg])
            dma("sp", nk_p, stg[:, 512:640], "outp", reads=[r_stg])
            dma("sp", nv_p, stg[:, 640:768], "outp", reads=[r_stg])
        if t == 0:
            mset("dve", uT_[:, :, 16:128], 0.0, [rp["uT"]])
        yield
        use(p)
        if t == 0:
            cp("dve", CTX[1 - p]["uT"][:, :, 0:16], uT_[:, :, 128:144], [rp["uT"]], [RP[1 - p]["uT"]])
            return
        kbs = [1] if t == 0 else [0, 1]
        for kvh in range(2):
            rows_ = slice(kvh * 64, kvh * 64 + 64)
            for kb in kbs:
                b_ = (2 if kvh == 0 else 6) + kb
                ksl = prv if kb == 0 else cur
                mm(bank(b_), kT[ksl], qzp[p][kvh], True, False, [r_kT[ksl], r_qz_], [BK[b_]])
                Bt_ = Bprev if kb == 0 else Bcur
                mm(bank(b_), ident_b, Bt_[:, kvh * 512:(kvh + 1) * 512], False, True, [r_c, r_B], [BK[b_]])
        for kvh in range(2):
            for kb in kbs:
                b_ = (2 if kvh == 0 else 6) + kb
                act(pTe_[kvh][kb], bank(b_), AF.Exp, [BK[b_]], [rp["pTe"]])
                if (t == 0 and kb == 1) or (t == 1 and kb == 0):
                    mset("dve", pTe_[kvh][kb][0:112, :], 0.0, [rp["pTe"]])
        PE_ = KPOOLENG
        window_sums(uT_, wsA, wsB, 144, rp["uT"], r_ws, eng=PE_)
        fin = [wsA, wsB, wsA, wsB]
        for g in range(4):
            stt("dve", ppT_[:, g, :], fin[g][:, g, 16:144], 1.0 / WIN[g], uT_[:, g, 16:144], ALU.mult, ALU.subtract,
                [r_ws, rp["uT"]], [rp["pp"]])
        if t == 0:
            r_dt = res("dtmp")
            cnt3 = rcnt.rearrange("p (g t) -> p g t", g=4)
            dt3 = dtmp.rearrange("p (g t) -> p g t", g=4)
            for g in range(4):
                tt("dve", dt3[:, g, :], fin[g][:, g, 128:144], cnt3[:, g, :], ALU.mult, [r_ws, res("rcnt")], [r_dt])
                tt("dve", ppT_[:, g, 112:128], dt3[:, g, :], uT_[:, g, 128:144], ALU.subtract, [r_dt, rp["uT"]], [rp["pp"]])
        if not last:
            cp(KPOOLENG, CTX[1 - p]["uT"][:, :, 0:16], uT_[:, :, 128:144], [rp["uT"]], [RP[1 - p]["uT"]])
        pool_mm(ppT_, 128, rp["pp"], rp["mix"])
        yield
        use(p)
        for kvh in range(2):
            rows_ = slice(kvh * 64, kvh * 64 + 64)
            for n_, kb in enumerate(kbs):
                vsl = prv if kb == 0 else cur
                mm(bank(4)[rows_, :], vtok[vsl][:, kvh * 64:(kvh + 1) * 64], pTe_[kvh][kb], n_ == 0, n_ == len(kbs) - 1,
                   [r_v[vsl], rp["pTe"]], [BK[4]])
            for n_, kb in enumerate(kbs):
                mm(bank(5)[rows_, :], ones_b[:, 0:64], pTe_[kvh][kb], n_ == 0, n_ == len(kbs) - 1,
                   [res("ones"), rp["pTe"]], [BK[5]])
        attn_finish(512, lambda a: a.rearrange("p (g t) -> p g t", g=4), mixT_[:, 4:8, :], rp["mix"])
        yield
        use(p)
        out_proj_ln1(Sap, 128, rS, rp["mix"], None if t == 0 else (t - 1) * 128, None if t == 0 else R_h1T[t - 1], defer_T=True)
        yield
        use(p)
        if t > 0:
            transpose_to(Sap, 128, rS, h1T3, (t - 1) * 128, R_h1T[t - 1])

    Z = Bump(persist_end)
    _wb10 = Z.bf16(8 * 1024).rearrange("p (k f) -> p k f", k=8)
    Z.f32(1024)
    _wb20 = Z.bf16(8 * 1024).rearrange("p (k f) -> p k f", k=8)
    _wb11 = Z.bf16(8 * 1024).rearrange("p (k f) -> p k f", k=8)
    _wb21 = Z.bf16(8 * 1024).rearrange("p (k f) -> p k f", k=8)
    Wb1 = [_wb10, _wb11]
    Wb2 = [_wb20, _wb21]
    G2 = Z.f32(1024); B2t = Z.f32(1024)
    stat2s = [Z.f32(16), Z.f32(16)]
    assert Z.off <= min(off_wsA, off_r1), (Z.off, off_wsA, off_r1)
    aT = [arena[:, off_r1:off_r1 + 2048].bitcast(BF16).rearrange("p (k t) -> p k t", k=8),
          arena[:, off_r2:off_r2 + 2048].bitcast(BF16).rearrange("p (k t) -> p k t", k=8)]
    rtmp = [arena[:, off_wsA:off_wsA + 512], arena[:, off_wsB:off_wsB + 512]]
    r_w1 = [res("wb1_0"), res("wb1_1")]; r_w2 = [res("wb2_0"), res("wb2_1")]
    r_aT = [res("aT0"), res("aT1")]; r_rt = [res("rt0"), res("rt1")]
    r_g2 = res("g2")
    w1_v = w1_d.rearrange("(k p) f -> p k f", p=128)
    w2_v = w2_d.rearrange("(k p) f -> p k f", p=128)


    def prefetch_w1q0():
        for kk in range(0, 8, 4):
            dma("pool", Wb1[0][:, kk:kk + 4, :], w1_v[:, kk:kk + 4, 0:1024], "w1q0", writes=[r_win, r_w1[0]] if kk == 0 else (),
                free_writes=() if kk == 0 else [r_w1[0]])

    def prefetch_w2q0():
        for kk in range(0, 8, 4):
            dma("pool", Wb2[0][:, kk:kk + 4, :], w2_v[:, kk:kk + 4, :], "w2q0", writes=[r_wout, r_w2[0]] if kk == 0 else (),
                free_writes=() if kk == 0 else [r_w2[0]])

    KPRE = _os.environ.get("KPRE", "1") == "1" and _os.environ.get("KDEBUG") != "1"
    NSTAGE = 7
    KORD = _os.environ.get("KORD", "old")
    SKEW = int(_os.environ.get("KSKEW", "1"))
    gens = [sample_gen()] + [tile_gen(t_) for t_ in range(NTILE + 1)]
    sample_done = False
    KS0 = int(_os.environ.get("KS0", "0"))
    sched = {}
    for j_ in range(NTILE + 2):
        for s_ in range(NSTAGE):
            st_ = SKEW * j_ + s_ if s_ >= 1 else max(0, SKEW * j_ - KS0)
            sched.setdefault(st_, []).append((j_, s_))
    for step in sorted(sched):
        for j_, s_ in sorted(sched[step]):
            try:
                next(gens[j_])
            except StopIteration:
                pass
            if KPRE and j_ == NTILE + 1 and s_ == 5:
                prefetch_w2q0()
            if KPRE and j_ == NTILE + 1 and s_ == 2:
                prefetch_w1q0()
            if j_ == 0 and s_ == NSTAGE - 1 and not sample_done:
                sample_done = True
                after_sample()
    use(0)

    if KSTOP == "C":
        return finish()
    DBG1 = _os.environ.get("KDEBUG") == "1"
    def load_quarter(q, after=()):
        sl = q % 2
        after = list(after)
        for kk in (range(0, 8, 4) if not (KPRE and q == 0) else ()):
            dma("pool", Wb1[sl][:, kk:kk + 4, :], w1_v[:, kk:kk + 4, q * 1024:(q + 1) * 1024], "w1q%d" % sl, reads=after,
                writes=[r_w1[sl]] if kk == 0 else (), free_writes=() if kk == 0 else [r_w1[sl]])
        for kk in (range(0, 8, 4) if not (KPRE and q == 0) else ()):
            dma("pool", Wb2[sl][:, kk:kk + 4, :], w2_v[:, q * 8 + kk:q * 8 + kk + 4, :], "w2q%d" % sl, writes=[r_w2[sl]] if kk == 0 else (),
                free_writes=() if kk == 0 else [r_w2[sl]])

    macros = [([(4 * m + i, 128) for i in range(4)], 4 * m * 128) for m in range(4)] + [([(16, NS)], 2048)]
    pre_in = {}
    KPREIN = int(_os.environ.get("KPREIN", "2"))
    KSQ = _os.environ.get("KSQ", "act")
    first_use = {"rt0": True, "rt1": True, "aT0": True, "aT1": True}

    def mlp_in(q, mi_):
        sl = q % 2
        tiles, tok0 = macros[mi_]
        n = sum(r for _, r in tiles)
        a_sl = mi_ % 2 if len(macros) % 2 == 0 else (q * len(macros) + mi_) % 2
        rh = [R_h1T[i] for i, _ in tiles]
        for fc in range(8):
            b_ = fc % 4
            for kc in range(8):
                mm(bank(b_, n), Wb1[sl][:, kc, fc * 128:(fc + 1) * 128], h1T3[:, kc, tok0:tok0 + n], kc == 0, kc == 7,
                   [r_w1[sl]] + rh, [BK[b_]])
            rt = fc % 2
            g_rt = []
            if first_use["rt%d" % rt]:
                first_use["rt%d" % rt] = False
                g_rt = [res("ws")]
            act(rtmp[rt][:, 0:n], bank(b_, n), AF.Relu, [BK[b_]], [r_rt[rt]] + g_rt)
            g_a = []
            if first_use["aT%d" % a_sl]:
                first_use["aT%d" % a_sl] = False
                g_a = ([RP[1]["h0T"], RP[1]["qT"], res("hb"), res("qz0"), res("qz1")] if a_sl == 0 else
                       [RP[1]["mix"], RP[1]["pTe"], res("stat_b"), res("rec_b"), RP[1]["pp"], RP[1]["uT"]])
            if KSQ == "act":
                act(aT[a_sl][:, fc, 0:n], rtmp[rt][:, 0:n], AF.Square, [r_rt[rt]], [r_aT[a_sl]] + g_a)
            else:
                tt("dve", aT[a_sl][:, fc, 0:n], rtmp[rt][:, 0:n], rtmp[rt][:, 0:n], ALU.mult, [r_rt[rt]], [r_aT[a_sl]] + g_a)
        return a_sl

    def mlp_out(q, mi_, a_sl):
        sl = q % 2
        tiles, tok0 = macros[mi_]
        for si, (idx, rows) in enumerate(tiles):
            Sap = Stile(idx)
            for half in range(2):
                b_ = 4 + (si % 2) * 2 + half
                for fc in range(8):
                    mm(bank(b_)[0:rows, :], aT[a_sl][:, fc, si * 128:si * 128 + rows], Wb2[sl][:, fc, half * 512:(half + 1) * 512],
                       fc == 0, fc == 7, [r_aT[a_sl], r_w2[sl]], [BK[b_]])
            for half in range(2):
                b_ = 4 + (si % 2) * 2 + half
                dst = Sap[0:rows, half * 512:(half + 1) * 512]
                if q == 0:
                    stt("dve", dst, dst, ALPHA, bank(b_)[0:rows, :], ALU.mult, ALU.add, [R_S[idx], BK[b_]], [R_S[idx]])
                else:
                    tt("dve", dst, dst, bank(b_)[0:rows, :], ALU.add, [R_S[idx], BK[b_]], [R_S[idx]])
            if q == 3:
                layernorm(Sap, rows, G2, B2t, R_S[idx], engine_gb="pool", stat_ap=stat2s[idx % 2], r_g=r_g2,
                          mode=("stt" if idx % 2 == 0 else "pool"))
                if idx < 16:
                    dma("sp", y_p[idx * 128:(idx + 1) * 128, :], Sap, "outy", reads=[R_S[idx]])
                else:
                    dma("sp", y_s, Sap[0:NS, :], "outy", reads=[R_S[idx]])

    if KPRE and not DBG1:
        for mi_ in range(KPREIN):
            pre_in[mi_] = mlp_in(0, mi_)
    pre_ops = [o for o in S.stream_ops.get("w1q0", [])] + [o for o in S.stream_ops.get("w2q0", [])]
    S.fence()
    if KPRE:
        for e_ in S.ENGS:
            S.fence_deps[e_] = {d for d in S.fence_deps[e_] if d not in pre_ops}
    dma("sp", G2, lnp_d[4:5, :].partition_broadcast(128), "g2", free_writes=[r_g2])
    dma("sp", B2t, lnp_d[5:6, :].partition_broadcast(128), "g2", free_writes=[r_g2])
    if not DBG1:
        load_quarter(0)
    for q in range(0 if DBG1 else 4):
        for mi_ in range(len(macros)):
            if q == 0 and mi_ in pre_in:
                a_sl = pre_in[mi_]
            else:
                a_sl = mlp_in(q, mi_)
            if mi_ == 0 and q + 1 < 4:
                load_quarter(q + 1, after=[r_aT[a_sl]])
            mlp_out(q, mi_, a_sl)
    S.fence()
    S.op("sp", lambda e: e.nop())

    names = list(Sched.ENGS) + list(S.stream_ops.keys())
    sems = {}
    for nm in names:
        sems[nm] = es.enter_context(nc.semaphore("s_" + nm))
    block = es.enter_context(nc.Block())
    S.emit(nc, block, sems)
    es.close()
    if DBG1:
        print("DBG offsets", dict(mixT=None))
        nc._dbg = dict(phase1_start=persist_end)
    return nc


_NC_CACHE = {}


def _get_nc():
    if "nc" not in _NC_CACHE:
        _NC_CACHE["nc"] = build_nc()
    return _NC_CACHE["nc"]


def kernel(x_prompt, x_sample, cache_win_k, cache_win_v, state_pool, meta_tokens, ln_emb_g, ln_emb_b,
           rel_table, w_in, w_pool, pool_scale, sinks, w_out, ln1_g, ln1_b, w_mlp_in, w_mlp_out, ln2_g, ln2_b):
    f = lambda a: np.ascontiguousarray(np.asarray(a, dtype=np.float32))
    x_prompt = f(x_prompt); x_sample = f(x_sample)
    ck = f(cache_win_k)[0].reshape(128, 128, 128)
    cv = f(cache_win_v)[0].reshape(128, 128, 128)
    sp = f(state_pool)[0]
    w_in0 = f(w_in)[0]
    qcols = []
    for c in range(4):
        qcols += list(range(512 + c * 64, 512 + c * 64 + 64)) + list(range(512 + (c + 4) * 64, 512 + (c + 4) * 64 + 64))
    cols = list(range(512)) + qcols + list(range(1024, 1280))
    win_p = np.ascontiguousarray(w_in0[:, cols])
    w_out0 = f(w_out)[0]
    rows = list(range(512)) + [r + 0 for r in qcols]
    wout_p = np.ascontiguousarray(w_out0[rows, :])
    lnp = np.stack([f(ln_emb_g), f(ln_emb_b), f(ln1_g)[0], f(ln1_b)[0], f(ln2_g)[0], f(ln2_b)[0]], axis=0)
    pscale = np.ascontiguousarray(f(pool_scale)[0].reshape(4, 128).T)
    sk = f(sinks)[0]
    sinkp = np.zeros((128, 4), np.float32)
    for g in range(4):
        sinkp[0:64, g] = sk[g]
        sinkp[64:128, g] = sk[4 + g]
    consts = _static_consts()
    common = dict(meta=f(meta_tokens), lnp=lnp, rel=f(rel_table), win=win_p, wpool=f(w_pool)[0], pscale=pscale,
                  sinkp=sinkp, wout=wout_p, w1=f(w_mlp_in)[0], w2=f(w_mlp_out)[0], **consts)
    in_maps = []
    for c in range(8):
        m = dict(common)
        m["xp"] = x_prompt[c]
        m["xs"] = x_sample[16 * c:16 * c + 16].reshape(NS, D)
        m["kc"] = ck[16 * c:16 * c + 16]
        m["vc"] = cv[16 * c:16 * c + 16]
        m["spl"] = sp[16 * c:16 * c + 16].reshape(NB * 15, 512)
        in_maps.append(m)
    nc = _get_nc()
    res_ = run_bass_kernel_spmd(nc, in_maps, core_ids=list(range(8)))
    rs = res_.results
    y_prompt = np.stack([rs[c]["y_p"] for c in range(8)], axis=0)
    y_sample = np.concatenate([rs[c]["y_s"].reshape(NB, TS, D) for c in range(8)], axis=0)
    nk_p = np.stack([rs[c]["nk_p"].reshape(128, 2, 64) for c in range(8)], axis=0)[None]
    nv_p = np.stack([rs[c]["nv_p"].reshape(128, 2, 64) for c in range(8)], axis=0)[None]
    np_p = np.stack([rs[c]["np_p"] for c in range(8)], axis=0)[None]
    nk_s = np.concatenate([rs[c]["nk_s"].reshape(NB, 128, 2, 64) for c in range(8)], axis=0)[None]
    nv_s = np.concatenate([rs[c]["nv_s"].reshape(NB, 128, 2, 64) for c in range(8)], axis=0)[None]
    np_s = np.concatenate([rs[c]["np_s"] for c in range(8)], axis=0)[None]
    return (y_prompt.astype(np.float32), y_sample.astype(np.float32), nk_p.astype(np.float32), nv_p.astype(np.float32),
            np_p.astype(np.float32), nk_s.astype(np.float32), nv_s.astype(np.float32), np_s.astype(np.float32))
```
